# Optimizing a Trainium2 kernel written in Bass

```python
import math
import jax, jax.numpy as jnp
from jax import lax
import numpy as np

D_MODEL = 4096
BATCH = 1
SEQ = 8192
DEPTH = 1

HEAD_DIM = 128
DILATION_GROUPS = ((128, 1), (512, 4), (2048, 16))
A_HEADS_PER_GROUP = 6
N_HEADS_A = A_HEADS_PER_GROUP * len(DILATION_GROUPS)
N_HEADS_B = D_MODEL // HEAD_DIM - N_HEADS_A
WIDTH_A = N_HEADS_A * HEAD_DIM
WIDTH_B = N_HEADS_B * HEAD_DIM
WIDTH_A_OUT = A_HEADS_PER_GROUP * HEAD_DIM
IN_COLS = 3 * WIDTH_A + 3 * WIDTH_B + N_HEADS_B + 2 * D_MODEL
Q_BLOCK = 128
ALIBI_MAX_EXP = 8.0
PEER_HEADS = 8
N_KEYS = 128
N_EXPERTS = N_KEYS * N_KEYS
PEER_TOPK = 16
PEER_QDIM = 256
PEER_QHALF = PEER_QDIM // 2
PEER_CHUNK = 64
EPS = 1e-6
NEG = -1e30

kernel_name = "hybrid_dilated_fox_peer_block"


def _rmsnorm(x, gain):
    x32 = x.astype(jnp.float32)
    y = x32 * lax.rsqrt(jnp.mean(x32 * x32, axis=-1, keepdims=True) + EPS)
    return (y * gain.astype(jnp.float32)).astype(x.dtype)


def _alibi_slopes(n):
    return jnp.exp2(-ALIBI_MAX_EXP * jnp.arange(1, n + 1, dtype=jnp.float32) / n)


def _dilated_group(q, k, v, slopes, window, dilation):
    B, S, H, Dh = q.shape
    L = window // dilation
    span = dilation * L
    S_pad = -(-S // span) * span
    M = S_pad // dilation
    nb = M // L

    def to_blocks(t):
        t = jnp.pad(t, ((0, 0), (0, S_pad - S), (0, 0), (0, 0)))
        t = t.reshape(B, M, dilation, H, Dh).transpose(0, 2, 1, 3, 4)
        return t.reshape(B, dilation, nb, L, H, Dh)

    def with_prev(t):
        prev = jnp.pad(t[:, :, :-1], ((0, 0), (0, 0), (1, 0), (0, 0), (0, 0), (0, 0)))
        return jnp.concatenate([prev, t], axis=3)

    qb = to_blocks(q)
    kc = with_prev(to_blocks(k))
    vc = with_prev(to_blocks(v))
    s = jnp.einsum('bcnqhd,bcnkhd->bcnqhk', qb, kc).astype(jnp.float32) * (HEAD_DIM ** -0.5)
    qi = jnp.arange(L)[:, None]
    kj = jnp.arange(2 * L)[None, :]
    delta = qi + L - kj
    band = (delta >= 0) & (delta <= L)
    first = (jnp.arange(nb)[:, None, None] == 0) & (kj[None] < L)
    valid = band[None] & (~first)
    bias = -(slopes * dilation)[None, :, None] * delta[:, None, :].astype(jnp.float32)
    s = jnp.where(valid[:, :, None, :], s + bias, NEG)
    m = jnp.max(s, axis=-1)
    p = jnp.exp(s - m[..., None])
    l = jnp.sum(p, axis=-1)
    o = jnp.einsum('bcnqhk,bcnkhd->bcnqhd', p, vc.astype(jnp.float32))

    def from_blocks(t):
        t = t.reshape((B, dilation, M) + t.shape[4:])
        t = jnp.swapaxes(t, 1, 2)
        return t.reshape((B, S_pad) + t.shape[3:])[:, :S]

    return from_blocks(o), from_blocks(m), from_blocks(l)


def _dilated_mixture(q, k, v):
    slopes = _alibi_slopes(N_HEADS_A)
    res = []
    for g, (window, dilation) in enumerate(DILATION_GROUPS):
        sl = slice(g * A_HEADS_PER_GROUP, (g + 1) * A_HEADS_PER_GROUP)
        res.append(_dilated_group(q[:, :, sl], k[:, :, sl], v[:, :, sl], slopes[sl], window, dilation))
    m_star = jnp.max(jnp.stack([r[1] for r in res], 0), axis=0)
    num = 0.0
    den = 0.0
    for o, m, l in res:
        w = jnp.exp(m - m_star)
        num = num + w[..., None] * o
        den = den + w * l
    return (num / den[..., None]).astype(q.dtype)


def _forgetting_attention(q, k, v, logf):
    B, S, H, Dh = q.shape
    c = jnp.cumsum(logf.astype(jnp.float32), axis=1).transpose(0, 2, 1)
    kpos = jnp.arange(S)
    nb = S // Q_BLOCK

    def block(i):
        start = i * Q_BLOCK
        qb = lax.dynamic_slice_in_dim(q, start, Q_BLOCK, axis=1)
        cq = lax.dynamic_slice_in_dim(c, start, Q_BLOCK, axis=2)
        s = jnp.einsum('bqhd,bkhd->bhqk', qb, k).astype(jnp.float32) * (HEAD_DIM ** -0.5)
        s = s + cq[..., :, None] - c[:, :, None, :]
        qpos = start + jnp.arange(Q_BLOCK)
        s = jnp.where(kpos[None, :] <= qpos[:, None], s, NEG)
        p = jax.nn.softmax(s, axis=-1)
        return jnp.einsum('bhqk,bkhd->bqhd', p.astype(v.dtype), v)

    o = lax.map(block, jnp.arange(nb))
    return o.transpose(1, 0, 2, 3, 4).reshape(B, S, H, Dh)


def _peer(h, w_q, subkeys, u, v):
    B, S, D = h.shape
    q = (h @ w_q).reshape(B, S, PEER_HEADS, 2, PEER_QHALF)
    sc = jnp.einsum('bshcd,hcnd->bshcn', q, subkeys).astype(jnp.float32)
    s1, i1 = lax.top_k(sc[..., 0, :], PEER_TOPK)
    s2, i2 = lax.top_k(sc[..., 1, :], PEER_TOPK)
    cand = (s1[..., :, None] + s2[..., None, :]).reshape(B, S, PEER_HEADS, PEER_TOPK * PEER_TOPK)
    cidx = (i1[..., :, None] * N_KEYS + i2[..., None, :]).reshape(B, S, PEER_HEADS, PEER_TOPK * PEER_TOPK)
    top, pos = lax.top_k(cand, PEER_TOPK)
    eidx = jnp.take_along_axis(cidx, pos, axis=-1)
    gate = jax.nn.softmax(top, axis=-1)
    T = B * S
    K = PEER_HEADS * PEER_TOPK
    nc = T // PEER_CHUNK
    hc = h.reshape(nc, PEER_CHUNK, D)
    ec = eidx.reshape(nc, PEER_CHUNK, K)
    gc = gate.reshape(nc, PEER_CHUNK, K).astype(h.dtype)

    def chunk(args):
        hx, ex, gx = args
        a = jnp.einsum('ckd,cd->ck', u[ex], hx)
        act = jax.nn.gelu(a, approximate=False) * gx
        return jnp.einsum('ck,ckd->cd', act, v[ex])

    out = lax.map(chunk, (hc, ec, gc))
    return out.reshape(B, S, D)


def setup_inputs(seed: int = 0) -> dict:
    key = jax.random.key(seed)
    ks = jax.random.split(key, 20)
    f32 = jnp.float32
    nrm = lambda k, shape, scale: jax.random.normal(k, shape, f32) * scale
    return {
        "x": nrm(ks[0], (BATCH, SEQ, D_MODEL), 1.0),
        "norm1_gain": 1.0 + nrm(ks[1], (DEPTH, D_MODEL), 0.02),
        "w_in": nrm(ks[2], (DEPTH, D_MODEL, IN_COLS), D_MODEL ** -0.5),
        "b_forget": 3.0 + nrm(ks[3], (DEPTH, N_HEADS_B), 0.1),
        "q_norm_a": 1.0 + nrm(ks[4], (DEPTH, HEAD_DIM), 0.02),
        "k_norm_a": 1.0 + nrm(ks[5], (DEPTH, HEAD_DIM), 0.02),
        "q_norm_b": 1.0 + nrm(ks[6], (DEPTH, HEAD_DIM), 0.02),
        "k_norm_b": 1.0 + nrm(ks[7], (DEPTH, HEAD_DIM), 0.02),
        "w_up_a": nrm(ks[8], (DEPTH, WIDTH_A_OUT, D_MODEL), WIDTH_A_OUT ** -0.5),
        "w_up_b": nrm(ks[9], (DEPTH, WIDTH_B, D_MODEL), WIDTH_B ** -0.5),
        "w_out": nrm(ks[10], (DEPTH, D_MODEL, D_MODEL), D_MODEL ** -0.5),
        "norm2_gain": 1.0 + nrm(ks[11], (DEPTH, D_MODEL), 0.02),
        "w_peer_q": nrm(ks[12], (DEPTH, D_MODEL, PEER_HEADS * PEER_QDIM), D_MODEL ** -0.5),
        "peer_subkeys": nrm(ks[13], (DEPTH, PEER_HEADS, 2, N_KEYS, PEER_QHALF), PEER_QHALF ** -0.5),
        "peer_u": nrm(ks[14], (DEPTH, N_EXPERTS, D_MODEL), D_MODEL ** -0.5),
        "peer_v": nrm(ks[15], (DEPTH, N_EXPERTS, D_MODEL), D_MODEL ** -0.5),
    }


def reference(x, norm1_gain, w_in, b_forget, q_norm_a, k_norm_a, q_norm_b, k_norm_b,
              w_up_a, w_up_b, w_out, norm2_gain, w_peer_q, peer_subkeys, peer_u, peer_v):
    B, S, D = x.shape
    sizes = [WIDTH_A, WIDTH_A, WIDTH_A, WIDTH_B, WIDTH_B, WIDTH_B, N_HEADS_B, D_MODEL, D_MODEL]
    cuts = [int(c) for c in np.cumsum(sizes)[:-1]]
    h = x
    for layer in range(DEPTH):
        xn = _rmsnorm(h, norm1_gain[layer])
        proj = xn @ w_in[layer]
        qa, ka, va, qb, kb, vb, f_logit, ga, gb = jnp.split(proj, cuts, axis=-1)
        qa = _rmsnorm(qa.reshape(B, S, N_HEADS_A, HEAD_DIM), q_norm_a[layer])
        ka = _rmsnorm(ka.reshape(B, S, N_HEADS_A, HEAD_DIM), k_norm_a[layer])
        va = va.reshape(B, S, N_HEADS_A, HEAD_DIM)
        qb = _rmsnorm(qb.reshape(B, S, N_HEADS_B, HEAD_DIM), q_norm_b[layer])
        kb = _rmsnorm(kb.reshape(B, S, N_HEADS_B, HEAD_DIM), k_norm_b[layer])
        vb = vb.reshape(B, S, N_HEADS_B, HEAD_DIM)
        logf = jax.nn.log_sigmoid(f_logit.astype(jnp.float32) + b_forget[layer].astype(jnp.float32))
        y_a = _dilated_mixture(qa, ka, va).reshape(B, S, WIDTH_A_OUT)
        y_b = _forgetting_attention(qb, kb, vb, logf).reshape(B, S, WIDTH_B)
        merged = jax.nn.sigmoid(ga) * (y_a @ w_up_a[layer]) + jax.nn.sigmoid(gb) * (y_b @ w_up_b[layer])
        h = h + merged @ w_out[layer]
        hn = _rmsnorm(h, norm2_gain[layer])
        h = h + _peer(hn, w_peer_q[layer], peer_subkeys[layer], peer_u[layer], peer_v[layer])
    return h
```

```python
import math
from contextlib import ExitStack

import numpy as np
import ml_dtypes

import concourse.bass as bass
import concourse.mybir as mybir
from concourse.bass_utils import run_bass_kernel_spmd

F32 = mybir.dt.float32
BF16 = mybir.dt.bfloat16
U32 = mybir.dt.uint32
AF = mybir.ActivationFunctionType
ALU = mybir.AluOpType
AX = mybir.AxisListType

NCORES = 8
D = 4096
S = 8192
TL = S // NCORES
NTT = TL // 128
NDC = D // 128
HD = 128
NHA, NHB = 18, 14
NH = NHA + NHB
WA, WB = NHA * HD, NHB * HD
IN_COLS = 3 * WA + 3 * WB + NHB + 2 * D
C_QA, C_KA, C_VA = 0, WA, 2 * WA
C_QB, C_KB, C_VB = 3 * WA, 3 * WA + WB, 3 * WA + 2 * WB
C_F = 3 * WA + 3 * WB
C_GA = C_F + NHB
C_GB = C_GA + D
EPS = 1e-6
NEG = -1e30
GROUPS = ((128, 1), (512, 4), (2048, 16))
PH, NK, TOPK, QH = 8, 128, 16, 128


class Buf:
    __slots__ = ("w", "r", "dsem", "dcnt", "name")

    def __init__(self, name=""):
        self.w = {}
        self.r = {}
        self.dsem = None
        self.dcnt = 0
        self.name = name


class Eng:
    def __init__(self, kb, name, h):
        self.h = h
        self.name = name
        self.sem = kb.new_sem("e_" + name)
        self.n = 0
        self.known = {}


class KB:
    def __init__(self, nc, es):
        self.nc = nc
        self.es = es
        self.nsem = 0
        self.pe = Eng(self, "pe", nc.tensor)
        self.act = Eng(self, "act", nc.scalar)
        self.dve = Eng(self, "dve", nc.vector)
        self.pool = Eng(self, "pool", nc.gpsimd)
        self.sp = Eng(self, "sp", nc.sync)
        self.uid = 0
        self.engs = [self.pe, self.act, self.dve, self.pool, self.sp]
        self.free_dsems = []
        self.live_dbufs = []
        self.phases = []

    @property
    def phase_es(self):
        return self.phases[-1][0] if self.phases else None

    def begin_phase(self):
        es = ExitStack()
        es.__enter__()
        self.phases.append((es, self.live_dbufs))
        self.live_dbufs = []

    def end_phase(self):
        for e in self.engs:
            for e2 in self.engs:
                if e2 is e or e2.n == 0:
                    continue
                k = id(e2.sem)
                if e.known.get(k, 0) < e2.n:
                    e.h.wait_ge(e2.sem, e2.n)
                    e.known[k] = e2.n
            allb = list(self.live_dbufs)
            for _, lst in self.phases:
                allb += lst
            for b in allb:
                k = id(b.dsem)
                if e.known.get(k, 0) < b.dcnt:
                    e.h.wait_ge(b.dsem, b.dcnt)
                    e.known[k] = b.dcnt
        for b in self.live_dbufs:
            self.free_dsems.append((b.dsem, b.dcnt))
            b.dsem = None
        es, outer = self.phases.pop()
        self.live_dbufs = outer
        es.__exit__(None, None, None)

    def new_sem(self, name):
        self.nsem += 1
        return self.es.enter_context(self.nc.semaphore(name))

    def sb(self, shape, dt, name=None):
        self.uid += 1
        return (self.phase_es or self.es).enter_context(self.nc.sbuf_tensor(f"{name or 'sb'}_{self.uid}", list(shape), dt))

    def ps(self, shape, dt, name=None):
        self.uid += 1
        return (self.phase_es or self.es).enter_context(self.nc.psum_tensor(f"{name or 'ps'}_{self.uid}", list(shape), dt))

    def _deps(self, reads, writes):
        deps = {}
        for b in reads:
            for k, (s, v) in b.w.items():
                if k not in deps or deps[k][1] < v:
                    deps[k] = (s, v)
        for b in writes:
            for dct in (b.w, b.r):
                for k, (s, v) in dct.items():
                    if k not in deps or deps[k][1] < v:
                        deps[k] = (s, v)
        return deps

    def _wait(self, eng, deps):
        for k, (s, v) in deps.items():
            if s is eng.sem and eng is self.pe:
                continue
            if eng.known.get(k, 0) >= v:
                continue
            eng.h.wait_ge(s, v)
            eng.known[k] = v

    def op(self, eng, fn, reads=(), writes=()):
        self._wait(eng, self._deps(reads, writes))
        ins = fn(eng.h)
        eng.n += 1
        ins.then_inc(eng.sem, 1)
        k = id(eng.sem)
        tok = (eng.sem, eng.n)
        eng.known[k] = max(eng.known.get(k, 0), 0)
        for b in writes:
            b.w[k] = tok
        for b in reads:
            b.r[k] = tok
        return ins

    def dma(self, q, out, in_, reads=(), writes=(), sembuf=None, **kw):
        self._wait(q, self._deps(reads, writes))
        sb_ = sembuf or (writes[0] if writes else reads[0])
        if sb_.dsem is None:
            if self.free_dsems:
                sb_.dsem, sb_.dcnt = self.free_dsems.pop()
            else:
                sb_.dsem = self.new_sem(f"d{self.nsem}")
                sb_.dcnt = 0
            self.live_dbufs.append(sb_)
        ins = q.h.dma_start(out=out, in_=in_, **kw)
        ins.then_inc(sb_.dsem, 16)
        sb_.dcnt += 16
        k = id(sb_.dsem)
        tok = (sb_.dsem, sb_.dcnt)
        for b in writes:
            b.w[k] = tok
        for b in reads:
            b.r[k] = tok
        return ins

    def drain(self, eng, bufs):
        self._wait(eng, self._deps(bufs, ()))


def make_identity(kb, dt):
    ident = kb.sb([128, 128], dt)
    b = Buf("ident")
    kb.op(kb.pool, lambda e: e.memset(ident[:], 0.0), writes=[b])
    kb.op(kb.pool, lambda e: e.affine_select(out=ident[:], in_=ident[:], pattern=[[-1, 128]],
                                             compare_op=ALU.not_equal, fill=1.0, base=0,
                                             channel_multiplier=1), reads=[b], writes=[b])
    return ident, b


def _nt_stage1(kb, x_ap, tt, xs_tiles, xs_bufs, junk, junk_b, stat, stat_b):
    xs, xb = xs_tiles[tt % 2], xs_bufs[tt % 2]
    kb.dma(kb.sp, xs[:], x_ap[tt * 128:(tt + 1) * 128, :], writes=[xb])
    st, sb_ = stat[tt % 2], stat_b[tt % 2]
    kb.op(kb.act, lambda e: e.activation(out=junk[:], in_=xs[:], func=AF.Square, accum_out=st[:, 0:1]),
          reads=[xb], writes=[junk_b, sb_])
    kb.op(kb.act, lambda e: e.activation(out=st[:, 1:2], in_=st[:, 0:1], func=AF.Sqrt, scale=1.0 / D, bias=EPS),
          reads=[sb_], writes=[sb_])
    kb.op(kb.dve, lambda e: e.reciprocal(out=st[:, 2:3], in_=st[:, 1:2]), reads=[sb_], writes=[sb_])
    kb.op(kb.act, lambda e: e.activation(out=xs[:], in_=xs[:], func=AF.Copy, scale=st[:, 2:3]),
          reads=[xb, sb_], writes=[xb])


def emit_norm_transpose(kb, x_ap, gcol, gcol_b, xnT, xnT_bufs, ident, ident_b, psT, psT_b, ntt,
                        xs_tiles, xs_bufs, junk, junk_b, stat, stat_b, staged=()):
    for tt in range(ntt):
        xs, xb = xs_tiles[tt % 2], xs_bufs[tt % 2]
        if tt not in staged:
            _nt_stage1(kb, x_ap, tt, xs_tiles, xs_bufs, junk, junk_b, stat, stat_b)
        for g in range(NDC // 4):
            p, pb = psT[g % 2], psT_b[g % 2]
            for i in range(4):
                dc = g * 4 + i
                kb.op(kb.pe, lambda e: e.transpose(p[:, i * 128:(i + 1) * 128],
                                                   xs[:, dc * 128:(dc + 1) * 128], ident[:]),
                      reads=[xb, ident_b], writes=[pb])
            kb.op(kb.dve, lambda e: e.tensor_tensor(
                out=xnT[:, g * 4:(g + 1) * 4, tt * 128:(tt + 1) * 128],
                in0=p[:].rearrange("p (a b) -> p a b", a=4),
                in1=gcol[:, g * 4:(g + 1) * 4].unsqueeze(2).to_broadcast([128, 4, 128]),
                op=ALU.mult), reads=[pb, gcol_b], writes=[xnT_bufs[tt]])


def alibi_slope(h):
    return 2.0 ** (-8.0 * (h + 1) / NHA)


def emit_attention(kb, qT, kall, vall, lfall, cst, yT_d, yT_db, heads_a=range(6), heads_b=range(NHB)):
    nc = kb.nc
    identb, identb_b = make_identity(kb, BF16)
    identf, identf_b = make_identity(kb, F32)
    def load_const(name, shape, dt, q=None):
        t = kb.sb(shape, dt, "c_" + name)
        b = Buf("c_" + name)
        kb.dma(q or kb.sp, t[:], cst[name], writes=[b])
        return t, b
    fmask, fmask_b = load_const("fmask", [128, 8 * 128], BF16)
    ndist, ndist_b = load_const("ndist", [128, 24 * 128], F32)
    vms = [load_const(f"vm{g}", [128, 24 * 128], F32) for g in range(3)]
    LT, LT_b = load_const("LT", [64, 8], F32)
    LTF, LTF_b = load_const("LTF", [64, 64], F32)
    i8e, i8e_b = load_const("i8e", [8, TL], BF16)
    ones8 = kb.sb([8, 128], BF16); ones8_b = Buf("ones8")
    kb.op(kb.pool, lambda e: e.memset(ones8[:], 1.0), writes=[ones8_b])
    onesf = kb.sb([128, 128], F32); onesf_b = Buf("onesf")
    kb.op(kb.pool, lambda e: e.memset(onesf[:], 1.0), writes=[onesf_b])
    zer = kb.sb([128, 512], BF16); zer_b = Buf("zer")
    kb.op(kb.pool, lambda e: e.memset(zer[:], 0.0), writes=[zer_b])
    tri = kb.sb([128, 128], F32); tri_b = Buf("tri")
    kb.op(kb.pool, lambda e: e.memset(tri[:], 1.0), writes=[tri_b])
    kb.op(kb.pool, lambda e: e.affine_select(out=tri[:], in_=tri[:], pattern=[[1, 128]], compare_op=ALU.is_ge,
                                             fill=0.0, base=0, channel_multiplier=-1), reads=[tri_b], writes=[tri_b])

    banks = [kb.ps([128, 512], F32, f"bank{i}") for i in range(7)]
    bank_b = [Buf(f"bank{i}") for i in range(7)]
    pst = kb.ps([128, 1024], BF16, "pst"); pst_b = Buf("pst")

    NB = S // 128
    LF = kb.sb([128, NB * NHB], F32); LF_b = Buf("LF")
    kb.dma(kb.sp, LF[:], lfall.rearrange("k t h -> k (t h)"), writes=[LF_b])
    HALF = NB * NHB // 2
    for hf in range(2):
        kb.op(kb.pe, lambda e: e.matmul(banks[hf][:, 0:HALF], lhsT=tri[:], rhs=LF[:, hf * HALF:(hf + 1) * HALF],
                                        start=True, stop=True), reads=[tri_b, LF_b], writes=[bank_b[hf]])
    call = kb.sb([128, NB * NHB], F32); call_b = Buf("call")
    srep = kb.sb([128, NB * NHB], F32); srep_b = Buf("srep")
    off = kb.sb([128, NB * NHB], F32); off_b = Buf("off")
    LF3 = LF[:].rearrange("p (b h) -> p b h", h=NHB)
    for h in range(NHB):
        kb.op(kb.pe, lambda e: e.matmul(banks[4][0:NB, h:h + 1], lhsT=LF3[:, :, h], rhs=onesf[:, 0:1], start=True, stop=True),
              reads=[LF_b, onesf_b], writes=[bank_b[4]])
    totT = kb.sb([NB, NHB], F32); totT_b = Buf("totT")
    kb.op(kb.act, lambda e: e.activation(out=totT[:], in_=banks[4][0:NB, 0:NHB], func=AF.Copy), reads=[bank_b[4]], writes=[totT_b])
    kb.op(kb.pe, lambda e: e.matmul(banks[6][0:NB, 0:NHB], lhsT=LTF[:], rhs=totT[:], start=True, stop=True),
          reads=[LTF_b, totT_b], writes=[bank_b[6]])
    offT = kb.sb([NB, NHB], F32); offT_b = Buf("offT")
    kb.op(kb.act, lambda e: e.activation(out=offT[:], in_=banks[6][0:NB, 0:NHB], func=AF.Copy), reads=[bank_b[6]], writes=[offT_b])
    Rm = kb.sb([NB, NB * NHB], F32); Rm_b = Buf("Rm")
    kb.op(kb.dve, lambda e: e.tensor_tensor(out=Rm[:].rearrange("p (b h) -> p b h", h=NHB),
                                            in0=identf[0:NB, 0:NB].unsqueeze(2).to_broadcast([NB, NB, NHB]),
                                            in1=offT[:].unsqueeze(1).to_broadcast([NB, NB, NHB]), op=ALU.mult),
          reads=[identf_b, offT_b], writes=[Rm_b])
    for hf in range(2):
        kb.op(kb.pe, lambda e: e.matmul(banks[2 + hf][:, 0:HALF], lhsT=onesf[0:NB, :], rhs=Rm[:, hf * HALF:(hf + 1) * HALF],
                                        start=True, stop=True), reads=[onesf_b, Rm_b], writes=[bank_b[2 + hf]])
        kb.op(kb.act, lambda e: e.activation(out=off[:, hf * HALF:(hf + 1) * HALF], in_=banks[2 + hf][:, 0:HALF], func=AF.Copy),
              reads=[bank_b[2 + hf]], writes=[off_b])
    for hf in range(2):
        kb.op(kb.dve, lambda e: e.tensor_tensor(out=call[:, hf * HALF:(hf + 1) * HALF], in0=banks[hf][:, 0:HALF],
                                                in1=off[:, hf * HALF:(hf + 1) * HALF], op=ALU.add),
              reads=[bank_b[hf], off_b], writes=[call_b])
    negc = kb.sb([128, NB * NHB], F32); negc_b = Buf("negc")
    kb.op(kb.dve, lambda e: e.tensor_scalar_mul(out=negc[:], in0=call[:], scalar1=-1.0), reads=[call_b], writes=[negc_b])
    negc_v = negc[:].rearrange("p (b h) -> p b h", h=NHB)
    kb.op(kb.pe, lambda e: e.matmul(banks[5][0:8, 0:NHB], lhsT=LT[:], rhs=totT[:], start=True, stop=True),
          reads=[LT_b, totT_b], writes=[bank_b[5]])
    offo = kb.sb([8, NHB], F32); offo_b = Buf("offo")
    kb.op(kb.act, lambda e: e.activation(out=offo[:], in_=banks[5][0:8, 0:NHB], func=AF.Copy), reads=[bank_b[5]], writes=[offo_b])
    crow = kb.sb([8, NHB * TL], BF16); crow_b = Buf("crow")
    for h in range(NHB):
        kb.op(kb.dve, lambda e: e.tensor_scalar_mul(out=crow[:, h * TL:(h + 1) * TL], in0=i8e[:], scalar1=offo[:, h:h + 1]),
              reads=[i8e_b, offo_b], writes=[crow_b])

    KT = [kb.sb([128, S], BF16, f"KT{i}") for i in range(2)]
    KT_b = [Buf(f"KT{i}") for i in range(2)]
    VA = [kb.sb([128, NB * 129], BF16, f"VA{i}") for i in range(2)]
    VA_b = [Buf(f"VA{i}") for i in range(2)]
    QT = [kb.sb([128, TL], BF16, f"QT{i}") for i in range(2)]
    QT_b = [Buf(f"QT{i}") for i in range(2)]
    for i in range(2):
        kb.op(kb.pool, lambda e: e.memset(VA[i][:], 1.0), writes=[VA_b[i]])
    PT = [kb.sb([128, 512], BF16, f"PT{i}") for i in range(2)]
    PT_b = [Buf(f"PT{i}") for i in range(2)]
    LG = [kb.sb([128, 512], F32, f"LG{i}") for i in range(2)]
    LG_b = [Buf(f"LG{i}") for i in range(2)]
    biash = kb.sb([128, 24 * 128], F32); biash_b = Buf("biash")
    ytok = [kb.sb([128, 128], BF16) for _ in range(2)]
    ytok_b = [Buf(f"ytok{i}") for i in range(2)]
    rden = [kb.sb([128, 1], F32) for _ in range(2)]
    rden_b = [Buf(f"rden{i}") for i in range(2)]
    yT = [kb.sb([128, TL], BF16, f"yT{i}") for i in range(2)]
    yT_b = [Buf(f"yTs{i}") for i in range(2)]

    seq = []
    for s_ in heads_a:
        for g in range(3):
            seq.append(("a", s_, g, 6 * g + s_))
    for h in heads_b:
        seq.append(("b", 6 + h, h, NHA + h))

    def load_head(i):
        kind, slot, g, kh = seq[i]
        p = i % 2
        nr = NEED_R[g] if kind == "a" else 8
        kb.dma(kb.sp, KT[p][:, 0:nr * TL], kall[kh][:, 0:nr * TL], writes=[KT_b[p]])
        kb.dma(kb.sp, VA[p][:].rearrange("p (t e) -> p t e", e=129)[:, 0:nr * NTT, 0:128],
               vall[kh][:, 0:nr * NTT, :], writes=[VA_b[p]])
        kb.dma(kb.sp, QT[p][:], qT[kh], writes=[QT_b[p]])

    cnt = {"u": 0, "y": 0, "yt": 0}

    def finish(acc_ap, acc_buf, slot_par, j):
        y = cnt["y"] % 2; cnt["y"] += 1
        kb.op(kb.dve, lambda e: e.reciprocal(out=rden[y][:], in_=acc_ap[:, 128:129]), reads=[acc_buf], writes=[rden_b[y]])
        kb.op(kb.dve, lambda e: e.tensor_scalar_mul(out=ytok[y][:], in0=acc_ap[:, 0:128], scalar1=rden[y][:, 0:1]),
              reads=[acc_buf, rden_b[y]], writes=[ytok_b[y]])
        kb.op(kb.pe, lambda e: e.transpose(pst[:, j * 128:(j + 1) * 128], ytok[y][:], identb[:]),
              reads=[ytok_b[y], identb_b], writes=[pst_b])
        kb.op(kb.act, lambda e: e.activation(out=yT[slot_par][:, j * 128:(j + 1) * 128], in_=pst[:, j * 128:(j + 1) * 128], func=AF.Copy),
              reads=[pst_b], writes=[yT_b[slot_par]])

    if seq:
        load_head(0)
    nslot = 0
    for i, (kind, slot, g, kh) in enumerate(seq):
        if i + 1 < len(seq):
            load_head(i + 1)
        p = i % 2
        kt, ktb, va, vab, qt, qtb = KT[p], KT_b[p], VA[p], VA_b[p], QT[p], QT_b[p]
        va3 = va[:].rearrange("p (t e) -> p t e", e=129)
        if kind == "a":
            accs = [banks[2 + j // 3][:, (j % 3) * 129:(j % 3) * 129 + 129] for j in range(8)]
            accs_b = [bank_b[2 + j // 3] for j in range(8)]
            if g == 0:
                for bk in range(3):
                    kb.op(kb.pe, lambda e: e.matmul(banks[2 + bk][:], lhsT=zer[:, 0:128], rhs=zer[:], start=True, stop=True),
                          reads=[zer_b], writes=[bank_b[2 + bk]])
            ndm = (2, 2, 3)[g]
            ntile = ndm * 8
            vm, vmb = vms[g]
            kb.op(kb.dve, lambda e: e.scalar_tensor_tensor(out=biash[:, 0:ntile * 128], in0=ndist[:, 0:ntile * 128],
                                                           scalar=float(alibi_slope(kh)), in1=vm[:, 0:ntile * 128],
                                                           op0=ALU.mult, op1=ALU.add),
                  reads=[ndist_b, vmb], writes=[biash_b])
            units = []
            for j in range(8):
                for dm in range(ndm):
                    if j - dm < 0:
                        continue
                    for k0 in range(0, NEED_R[g], 4):
                        units.append((j, [(dm, c_) for c_ in range(k0, min(k0 + 4, NEED_R[g]))]))

            def qk(u):
                j, tl = units[u]
                sb_i = cnt["u"] % 2
                for ii, (dm, c_) in enumerate(tl):
                    col = (c_ * 8 + (j - dm)) * 128
                    kb.op(kb.pe, lambda e: e.matmul(banks[sb_i][:, ii * 128:(ii + 1) * 128], lhsT=kt[:, col:col + 128],
                                                    rhs=qt[:, j * 128:(j + 1) * 128], start=True, stop=True),
                          reads=[ktb, qtb], writes=[bank_b[sb_i]])
                n = len(tl) * 128
                b0 = (tl[0][0] * 8 + tl[0][1]) * 128
                kb.op(kb.dve, lambda e: e.tensor_tensor(out=LG[sb_i][:, 0:n], in0=banks[sb_i][:, 0:n], in1=biash[:, b0:b0 + n], op=ALU.add),
                      reads=[bank_b[sb_i], biash_b], writes=[LG_b[sb_i]])
                kb.op(kb.act, lambda e: e.activation(out=PT[sb_i][:, 0:n], in_=LG[sb_i][:, 0:n], func=AF.Exp),
                      reads=[LG_b[sb_i]], writes=[PT_b[sb_i]])
                cnt["u"] += 1
                return sb_i

            def pv(u, sb_i):
                j, tl = units[u]
                for ii, (dm, c_) in enumerate(tl):
                    t_ = c_ * 8 + (j - dm)
                    kb.op(kb.pe, lambda e: e.matmul(accs[j], lhsT=PT[sb_i][:, ii * 128:(ii + 1) * 128], rhs=va3[:, t_, :],
                                                    start=False, stop=False),
                          reads=[PT_b[sb_i], vab], writes=[accs_b[j]])
            prev = qk(0)
            for u in range(len(units)):
                nxt = qk(u + 1) if u + 1 < len(units) else None
                pv(u, prev)
                prev = nxt
            if g == 2:
                sp_ = nslot % 2; nslot += 1
                for j in range(8):
                    finish(accs[j], accs_b[j], sp_, j)
                kb.dma(kb.sp, yT_d[slot], yT[sp_][:], reads=[yT_b[sp_]], writes=[yT_db], sembuf=yT_b[sp_])
        else:
            h = g
            sp_ = nslot % 2; nslot += 1
            for G in range(2):
                i0 = 4 * G
                accs = [banks[2 + a][:, 0:129] for a in range(4)]
                accs_b = [bank_b[2 + a] for a in range(4)]
                tiles = [(m, c_) for m in range(i0 + 4) for c_ in range(8)]

                def qk(t):
                    m, c_ = tiles[t]
                    sb_i = cnt["u"] % 2
                    ilo = max(m, i0)
                    c0, c1 = (ilo - i0) * 128, 512
                    col = (c_ * 8 + m) * 128
                    last_is_mask = m >= i0
                    kb.op(kb.pe, lambda e: e.matmul(banks[sb_i][:, c0:c1], lhsT=kt[:, col:col + 128],
                                                    rhs=qt[:, ilo * 128:(i0 + 4) * 128], start=True, stop=False),
                          reads=[ktb, qtb], writes=[bank_b[sb_i]])
                    kb.op(kb.pe, lambda e: e.matmul(banks[sb_i][:, c0:c1], lhsT=ones8[:],
                                                    rhs=crow[:, h * TL + ilo * 128:h * TL + (i0 + 4) * 128],
                                                    start=False, stop=not last_is_mask),
                          reads=[ones8_b, crow_b], writes=[bank_b[sb_i]])
                    if last_is_mask:
                        kb.op(kb.pe, lambda e: e.matmul(banks[sb_i][:, c0:c0 + 128], lhsT=identb[:],
                                                        rhs=fmask[:, c_ * 128:(c_ + 1) * 128], start=False, stop=True),
                              reads=[identb_b, fmask_b], writes=[bank_b[sb_i]])
                    kb.op(kb.act, lambda e: e.activation(out=PT[sb_i][:, c0:c1], in_=banks[sb_i][:, c0:c1], func=AF.Exp,
                                                         bias=negc_v[:, c_ * 8 + m, h:h + 1]),
                          reads=[bank_b[sb_i], negc_b], writes=[PT_b[sb_i]])
                    cnt["u"] += 1
                    return sb_i

                def pv(t, sb_i):
                    m, c_ = tiles[t]
                    t_ = c_ * 8 + m
                    for ii in range(max(m, i0), i0 + 4):
                        a = ii - i0
                        kb.op(kb.pe, lambda e: e.matmul(accs[a], lhsT=PT[sb_i][:, a * 128:(a + 1) * 128], rhs=va3[:, t_, :],
                                                        start=(t == 0), stop=(m == ii and c_ == 7)),
                              reads=[PT_b[sb_i], vab], writes=[accs_b[a]])
                prev = qk(0)
                for t in range(len(tiles)):
                    nxt = qk(t + 1) if t + 1 < len(tiles) else None
                    pv(t, prev)
                    prev = nxt
                for a in range(4):
                    finish(accs[a], accs_b[a], sp_, i0 + a)
            kb.dma(kb.sp, yT_d[slot], yT[sp_][:], reads=[yT_b[sp_]], writes=[yT_db], sembuf=yT_b[sp_])


def core_consts(c):
    k = np.arange(128)[:, None, None]
    q = np.arange(128)[None, None, :]
    r8 = np.arange(8)
    cp8 = (c - r8) % 8
    cp = cp8[None, :, None]
    fm = np.where((cp < c) | ((cp == c) & (k <= q)), 0.0, NEG).astype(np.float32)
    fmask = np.ascontiguousarray(fm.reshape(128, 8 * 128)).astype(ml_dtypes.bfloat16)
    dm = np.repeat(np.arange(3), 8)[None, :, None]
    cc = np.tile(cp8, 3)[None, :, None]
    dist = (8 * dm + c - cc) * 128 + q - k
    out = {"fmask": fmask, "ndist": np.ascontiguousarray((-dist).astype(np.float32).reshape(128, 24 * 128))}
    for g, (win, r) in enumerate(GROUPS):
        ok = (dist >= 0) & (dist <= win) & (dist % r == 0)
        out[f"vm{g}"] = np.ascontiguousarray(np.where(ok, 0.0, NEG).astype(np.float32).reshape(128, 24 * 128))
    glob = (cp8[:, None] + 8 * np.arange(8)[None, :]).reshape(64)
    j = np.arange(8)[None, :]
    out["LT"] = np.ascontiguousarray((glob[:, None] < 8 * j + c).astype(np.float32))
    out["LTF"] = np.ascontiguousarray((glob[:, None] < glob[None, :]).astype(np.float32))
    i8 = np.zeros((8, 8, 128), np.float32)
    for jj in range(8):
        i8[jj, jj] = 1.0
    out["i8e"] = np.ascontiguousarray(i8.reshape(8, TL)).astype(ml_dtypes.bfloat16)
    return out


CONST_SPECS = (("fmask", [128, 1024], BF16), ("ndist", [128, 3072], F32), ("vm0", [128, 3072], F32),
               ("vm1", [128, 3072], F32), ("vm2", [128, 3072], F32), ("LT", [64, 8], F32), ("LTF", [64, 64], F32), ("i8e", [8, TL], BF16))


def emit_mid(kb, x, g1, w_g, w_up_a, w_up_b, w_out, g2, yT_d, yT_db, merged_d, h_d, h_db, hnT_d, hnT_db):
    nc = kb.nc
    merged_db = Buf("merged_d")
    wua_v = w_up_a.rearrange("(s p) c -> p s c", p=128)
    wub_v = w_up_b.rearrange("(s p) c -> p s c", p=128)
    wo_v = w_out.rearrange("(fc p) c -> p fc c", p=128)
    kb.begin_phase()
    ident, ident_b = make_identity(kb, F32)
    gcol = kb.sb([128, NDC], F32); gcol_b = Buf("gcol")
    kb.dma(kb.sp, gcol[:], g1[:, :], writes=[gcol_b])
    xnT = kb.sb([128, NDC, TL], BF16, "xnT")
    xnT_bufs = [Buf(f"xnT{t}") for t in range(NTT)]
    yT = kb.sb([128, 20, TL], BF16, "yTall"); yT_b = Buf("yTall")
    yT_src = yT_d.rearrange("s p t -> p s t")
    for lo, hi in ((0, 10), (10, 20)):
        kb.dma(kb.sp, yT[:, lo:hi, :], yT_src[:, lo:hi, :], reads=[yT_db], writes=[yT_b])
    kb.begin_phase()
    xs_tiles = [kb.sb([128, D], F32) for _ in range(2)]
    xs_bufs = [Buf(f"xs{i}") for i in range(2)]
    junk = kb.sb([128, D], BF16); junk_b = Buf("junk")
    stat = [kb.sb([128, 4], F32) for _ in range(2)]
    stat_b = [Buf(f"stat{i}") for i in range(2)]
    psT = [kb.ps([128, 512], F32) for _ in range(2)]
    psT_b = [Buf(f"psT{i}") for i in range(2)]
    emit_norm_transpose(kb, x, gcol, gcol_b, xnT, xnT_bufs, ident, ident_b, psT, psT_b, NTT,
                        xs_tiles, xs_bufs, junk, junk_b, stat, stat_b)
    kb.end_phase()
    CW = 256
    wga = [kb.sb([128, NDC, CW], BF16, f"wga{i}") for i in range(2)]
    wgb = [kb.sb([128, NDC, CW], BF16, f"wgb{i}") for i in range(2)]
    wua = [kb.sb([128, 6, CW], BF16, f"wua{i}") for i in range(2)]
    wub = [kb.sb([128, 14, CW], BF16, f"wub{i}") for i in range(2)]
    wg_b = [Buf(f"wgrp{i}") for i in range(2)]
    pg = [[kb.ps([128, 512], F32) for _ in range(4)] for _ in range(2)]
    pg_b = [[Buf(f"pg{i}{k}") for k in range(4)] for i in range(2)]
    sg = [[kb.sb([128, 512], F32) for _ in range(2)] for _ in range(2)]
    sg_b = [[Buf(f"sg{i}{k}") for k in range(2)] for i in range(2)]
    mst = [kb.sb([128, 512], BF16) for _ in range(4)]
    mst_b = [Buf(f"mst{i}") for i in range(4)]

    def load_grp(gi):
        p = gi % 2
        c0 = gi * CW
        half = NDC * CW
        for part in range(2):
            sl = slice(part * 16, (part + 1) * 16)
            kb.dma(kb.pool, wga[p][:, sl, :], w_g[gi][:, part * 16 * CW:(part + 1) * 16 * CW].rearrange("p (a b) -> p a b", b=CW), writes=[wg_b[p]])
            kb.dma(kb.pool, wgb[p][:, sl, :], w_g[gi][:, half + part * 16 * CW:half + (part + 1) * 16 * CW].rearrange("p (a b) -> p a b", b=CW), writes=[wg_b[p]])
        kb.dma(kb.pool, wua[p][:], wua_v[:, :, c0:c0 + CW], writes=[wg_b[p]])
        kb.dma(kb.pool, wub[p][:], wub_v[:, :, c0:c0 + CW], writes=[wg_b[p]])

    NG = D // CW
    load_grp(0)
    it = 0
    for gi in range(NG):
        if gi + 1 < NG:
            load_grp(gi + 1)
        p = gi % 2
        for fcl in range(CW // 128):
            fc = gi * (CW // 128) + fcl
            cs = slice(fcl * 128, (fcl + 1) * 128)
            for th in range(TL // 512):
                q = it % 2; it += 1
                ts = slice(th * 512, (th + 1) * 512)
                xb = xnT_bufs[th * 4:(th + 1) * 4]
                for dc in range(NDC):
                    kb.op(kb.pe, lambda e: e.matmul(pg[q][0][:], lhsT=wga[p][:, dc, cs], rhs=xnT[:, dc, ts],
                                                    start=(dc == 0), stop=(dc == NDC - 1)), reads=[wg_b[p]] + xb, writes=[pg_b[q][0]])
                for dc in range(NDC):
                    kb.op(kb.pe, lambda e: e.matmul(pg[q][1][:], lhsT=wgb[p][:, dc, cs], rhs=xnT[:, dc, ts],
                                                    start=(dc == 0), stop=(dc == NDC - 1)), reads=[wg_b[p]] + xb, writes=[pg_b[q][1]])
                for sl in range(6):
                    kb.op(kb.pe, lambda e: e.matmul(pg[q][2][:], lhsT=wua[p][:, sl, cs], rhs=yT[:, sl, ts],
                                                    start=(sl == 0), stop=(sl == 5)), reads=[wg_b[p], yT_b], writes=[pg_b[q][2]])
                for sl in range(14):
                    kb.op(kb.pe, lambda e: e.matmul(pg[q][3][:], lhsT=wub[p][:, sl, cs], rhs=yT[:, 6 + sl, ts],
                                                    start=(sl == 0), stop=(sl == 13)), reads=[wg_b[p], yT_b], writes=[pg_b[q][3]])
                for k in range(2):
                    kb.op(kb.act, lambda e: e.activation(out=sg[q][k][:], in_=pg[q][k][:], func=AF.Sigmoid),
                          reads=[pg_b[q][k]], writes=[sg_b[q][k]])
                    kb.op(kb.dve, lambda e: e.tensor_tensor(out=sg[q][k][:], in0=sg[q][k][:], in1=pg[q][2 + k][:], op=ALU.mult),
                          reads=[sg_b[q][k], pg_b[q][2 + k]], writes=[sg_b[q][k]])
                m_ = it % 4
                kb.op(kb.dve, lambda e: e.tensor_tensor(out=mst[m_][:], in0=sg[q][0][:], in1=sg[q][1][:], op=ALU.add),
                      reads=[sg_b[q][0], sg_b[q][1]], writes=[mst_b[m_]])
                kb.dma(kb.sp, merged_d[fc, :, ts], mst[m_][:], reads=[mst_b[m_]], writes=[merged_db], sembuf=mst_b[m_])
    kb.end_phase()

    kb.begin_phase()
    mT = kb.sb([128, NDC, TL], BF16, "mT"); mT_b = Buf("mT")
    mT_src = merged_d.rearrange("f p t -> p f t")
    for lo in (0, NDC // 2):
        kb.dma(kb.sp, mT[:, lo:lo + NDC // 2, :], mT_src[:, lo:lo + NDC // 2, :], reads=[merged_db], writes=[mT_b])
    wo = [kb.sb([128, NDC, 512], BF16, f"wo{i}") for i in range(2)]
    wo_b = [Buf(f"wo{i}") for i in range(2)]
    po = [kb.ps([128, 512], F32) for _ in range(4)]
    po_b = [Buf(f"po{i}") for i in range(4)]
    xr = [kb.sb([128, 512], F32) for _ in range(4)]
    xr_b = [Buf(f"xr{i}") for i in range(4)]

    def load_wo(cb):
        for part in range(4):
            kb.dma(kb.pool, wo[cb % 2][:, part * 8:(part + 1) * 8, :], wo_v[:, part * 8:(part + 1) * 8, cb * 512:(cb + 1) * 512],
                   writes=[wo_b[cb % 2]])
    load_wo(0)
    it = 0
    for cb in range(D // 512):
        if cb + 1 < D // 512:
            load_wo(cb + 1)
        for tt in range(NTT):
            q = it % 4; it += 1
            kb.dma(kb.sp, xr[q][:], x[tt * 128:(tt + 1) * 128, cb * 512:(cb + 1) * 512], writes=[xr_b[q]])
            for fc in range(NDC):
                kb.op(kb.pe, lambda e: e.matmul(po[q][:], lhsT=mT[:, fc, tt * 128:(tt + 1) * 128], rhs=wo[cb % 2][:, fc, :],
                                                start=(fc == 0), stop=(fc == NDC - 1)), reads=[mT_b, wo_b[cb % 2]], writes=[po_b[q]])
            kb.op(kb.dve, lambda e: e.tensor_tensor(out=xr[q][:], in0=po[q][:], in1=xr[q][:], op=ALU.add),
                  reads=[po_b[q], xr_b[q]], writes=[xr_b[q]])
            kb.dma(kb.sp, h_d[tt * 128:(tt + 1) * 128, cb * 512:(cb + 1) * 512], xr[q][:], reads=[xr_b[q]], writes=[h_db], sembuf=xr_b[q])
    kb.end_phase()

    kb.begin_phase()
    ident, ident_b = make_identity(kb, F32)
    gcol = kb.sb([128, NDC], F32); gcol_b = Buf("gcol2")
    kb.dma(kb.sp, gcol[:], g2[:, :], writes=[gcol_b])
    hnT = kb.sb([128, NDC, TL], BF16, "hnT")
    hnT_bufs = [Buf(f"hnT{t}") for t in range(NTT)]
    xs_tiles = [kb.sb([128, D], F32) for _ in range(2)]
    xs_bufs = [Buf(f"hs{i}") for i in range(2)]
    junk = kb.sb([128, D], BF16); junk_b = Buf("junk")
    stat = [kb.sb([128, 4], F32) for _ in range(2)]
    stat_b = [Buf(f"stat{i}") for i in range(2)]
    psT = [kb.ps([128, 512], F32) for _ in range(2)]
    psT_b = [Buf(f"psT{i}") for i in range(2)]
    kb.drain(kb.sp, [h_db])
    emit_norm_transpose(kb, h_d, gcol, gcol_b, hnT, hnT_bufs, ident, ident_b, psT, psT_b, NTT,
                        xs_tiles, xs_bufs, junk, junk_b, stat, stat_b)
    for dc in range(NDC):
        kb.dma(kb.sp, hnT_d[dc], hnT[:, dc, :], reads=hnT_bufs, writes=[hnT_db], sembuf=hnT_bufs[0])
    kb.end_phase()


def emit_peer(kb, hnT_d, hnT_db, h_d, h_db, w_pq, skT, uT, v, iota_d, sc_d, G_d, act_d, y, y_db, NJ=128):
    nc = kb.nc
    wq_v = w_pq.rearrange("(dc p) c -> p dc c", p=128)
    sc_db, G_db = Buf("sc_d"), Buf("G_d")
    kb.begin_phase()
    identf, identf_b = make_identity(kb, F32)
    stT = kb.sb([128, 4, TL], F32, "stT"); stT_b = Buf("stT")

    kb.begin_phase()
    qpT = kb.sb([128, 16, TL], BF16, "qpT"); qpT_b = Buf("qpT")
    skb = kb.sb([128, 16, 128], BF16, "skb"); skb_b = Buf("skb")
    kb.dma(kb.pool, skb[:], skT.rearrange("k q n -> q k n"), writes=[skb_b])
    kb.begin_phase()
    hnT = kb.sb([128, NDC, TL], BF16, "hnT"); hnT_b = Buf("hnT")
    hnT_src = hnT_d.rearrange("d p t -> p d t")
    for lo in (0, NDC // 2):
        kb.dma(kb.sp, hnT[:, lo:lo + NDC // 2, :], hnT_src[:, lo:lo + NDC // 2, :], reads=[hnT_db], writes=[hnT_b])
    wq = [kb.sb([128, NDC, 512], BF16, f"wq{i}") for i in range(2)]
    wq_b = [Buf(f"wq{i}") for i in range(2)]
    pq = [kb.ps([128, 512], F32) for _ in range(2)]
    pq_b = [Buf(f"pq{i}") for i in range(2)]

    def load_wq(g):
        for part in range(4):
            kb.dma(kb.pool, wq[g % 2][:, part * 8:(part + 1) * 8, :], wq_v[:, part * 8:(part + 1) * 8, g * 512:(g + 1) * 512],
                   writes=[wq_b[g % 2]])
    load_wq(0)
    it = 0
    for g in range(4):
        if g + 1 < 4:
            load_wq(g + 1)
        for kk in range(4):
            k = g * 4 + kk
            for th in range(TL // 512):
                q = it % 2; it += 1
                for dc in range(NDC):
                    kb.op(kb.pe, lambda e: e.matmul(pq[q][:], lhsT=wq[g % 2][:, dc, kk * 128:(kk + 1) * 128],
                                                    rhs=hnT[:, dc, th * 512:(th + 1) * 512], start=(dc == 0), stop=(dc == NDC - 1)),
                          reads=[wq_b[g % 2], hnT_b], writes=[pq_b[q]])
                kb.op(kb.act, lambda e: e.activation(out=qpT[:, k, th * 512:(th + 1) * 512], in_=pq[q][:], func=AF.Copy),
                      reads=[pq_b[q]], writes=[qpT_b])
    kb.end_phase()

    ps_sc = [kb.ps([128, 512], F32) for _ in range(4)]
    ps_sc_b = [Buf(f"pssc{i}") for i in range(4)]
    ps_t = kb.ps([128, 512], F32); ps_t_b = Buf("pst")
    sc = [kb.sb([128, 16 * 128], F32, f"sc{i}") for i in range(2)]
    sc_b = [Buf(f"sc{i}") for i in range(2)]
    t16 = kb.sb([128, 16 * 16], F32); t16_b = Buf("t16")
    tmp = [kb.sb([128, 128], F32) for _ in range(2)]
    tmp_b = [Buf(f"tmp{i}") for i in range(2)]
    idx = kb.sb([128, 8 * 16], U32); idx_b = Buf("idx")
    cand = kb.sb([128, 8 * 256], F32); cand_b = Buf("cand")
    c16 = kb.sb([128, 8 * 16], F32); c16_b = Buf("c16")
    tmp2 = [kb.sb([128, 256], F32) for _ in range(2)]
    tmp2_b = [Buf(f"tmp2{i}") for i in range(2)]
    d16 = kb.sb([128, 128], F32); d16_b = Buf("d16")
    zs = kb.sb([128, 32], F32); zs_b = Buf("zs")
    pk = kb.sb([128, 4 * 128], F32); pk_b = Buf("pk")
    for tt in range(NTT):
        s_, sb_ = sc[tt % 2], sc_b[tt % 2]
        for k in range(16):
            kb.op(kb.pe, lambda e: e.matmul(ps_sc[k // 4][:, (k % 4) * 128:(k % 4 + 1) * 128], lhsT=qpT[:, k, tt * 128:(tt + 1) * 128],
                                            rhs=skb[:, k, :], start=True, stop=True), reads=[qpT_b, skb_b], writes=[ps_sc_b[k // 4]])
        for q in range(4):
            kb.op(kb.act, lambda e: e.activation(out=s_[:, q * 512:(q + 1) * 512], in_=ps_sc[q][:], func=AF.Copy),
                  reads=[ps_sc_b[q]], writes=[sb_])
        sc3 = s_[:].rearrange("p (k n) -> p k n", n=128)
        kb.dma(kb.sp, sc_d[tt * 128:(tt + 1) * 128],
               s_[:].rearrange("p (h c n) -> p h c n", c=2, n=128)[:, :, 1, :], reads=[sb_], writes=[sc_db], sembuf=sb_)
        t163 = t16[:].rearrange("p (k a) -> p k a", a=16)
        idx3 = idx[:].rearrange("p (h a) -> p h a", a=16)
        for k in range(16):
            tm, tmb = tmp[k % 2], tmp_b[k % 2]
            kb.op(kb.dve, lambda e: e.max(out=t163[:, k, 0:8], in_=sc3[:, k, :]), reads=[sb_], writes=[t16_b])
            kb.op(kb.dve, lambda e: e.match_replace(out=tm[:], in_to_replace=t163[:, k, 0:8], in_values=sc3[:, k, :], imm_value=-1e30),
                  reads=[sb_, t16_b], writes=[tmb])
            kb.op(kb.dve, lambda e: e.max(out=t163[:, k, 8:16], in_=tm[:]), reads=[tmb], writes=[t16_b])
            if k % 2 == 0:
                kb.op(kb.dve, lambda e: e.max_index(out=idx3[:, k // 2, 0:8], in_max=t163[:, k, 0:8], in_values=sc3[:, k, :]),
                      reads=[sb_, t16_b], writes=[idx_b])
                kb.op(kb.dve, lambda e: e.max_index(out=idx3[:, k // 2, 8:16], in_max=t163[:, k, 8:16], in_values=tm[:]),
                      reads=[tmb, t16_b], writes=[idx_b])
        t16v = t16[:].rearrange("p (h c a) -> p h c a", c=2, a=16)
        kb.op(kb.dve, lambda e: e.tensor_tensor(out=cand[:].rearrange("p (h a b) -> p h a b", a=16, b=16),
                                                in0=t16v[:, :, 0, :].unsqueeze(3).to_broadcast([128, 8, 16, 16]),
                                                in1=t16v[:, :, 1, :].unsqueeze(2).to_broadcast([128, 8, 16, 16]), op=ALU.add),
              reads=[t16_b], writes=[cand_b])
        cand3 = cand[:].rearrange("p (h n) -> p h n", n=256)
        c163 = c16[:].rearrange("p (h a) -> p h a", a=16)
        for h in range(8):
            tm, tmb = tmp2[h % 2], tmp2_b[h % 2]
            kb.op(kb.dve, lambda e: e.max(out=c163[:, h, 0:8], in_=cand3[:, h, :]), reads=[cand_b], writes=[c16_b])
            kb.op(kb.dve, lambda e: e.match_replace(out=tm[:], in_to_replace=c163[:, h, 0:8], in_values=cand3[:, h, :], imm_value=-1e30),
                  reads=[cand_b, c16_b], writes=[tmb])
            kb.op(kb.dve, lambda e: e.max(out=c163[:, h, 8:16], in_=tm[:]), reads=[tmb], writes=[c16_b])
        kb.op(kb.dve, lambda e: e.tensor_tensor(out=d16[:].rearrange("p (h a) -> p h a", a=16), in0=c163,
                                                in1=c163[:, :, 0:1].to_broadcast([128, 8, 16]), op=ALU.subtract),
              reads=[c16_b], writes=[d16_b])
        kb.op(kb.act, lambda e: e.activation(out=d16[:], in_=d16[:], func=AF.Exp), reads=[d16_b], writes=[d16_b])
        kb.op(kb.dve, lambda e: e.reduce_sum(out=zs[:, 0:8], in_=d16[:].rearrange("p (h a) -> p h a", a=16), axis=AX.X),
              reads=[d16_b], writes=[zs_b])
        kb.op(kb.act, lambda e: e.activation(out=zs[:, 8:16], in_=zs[:, 0:8], func=AF.Ln), reads=[zs_b], writes=[zs_b])
        kb.op(kb.dve, lambda e: e.tensor_tensor(out=zs[:, 16:24], in0=c163[:, :, 0], in1=zs[:, 8:16], op=ALU.add),
              reads=[c16_b, zs_b], writes=[zs_b])
        pk4 = pk[:].rearrange("p (q h a) -> p q h a", q=4, a=16)
        kb.op(kb.dve, lambda e: e.tensor_copy(out=pk4[:, 0], in_=t16v[:, :, 0, :]), reads=[t16_b], writes=[pk_b])
        kb.op(kb.dve, lambda e: e.tensor_copy(out=pk4[:, 1], in_=c163[:, :, 15:16].to_broadcast([128, 8, 16])),
              reads=[c16_b], writes=[pk_b])
        kb.op(kb.dve, lambda e: e.tensor_copy(out=pk4[:, 2], in_=zs[:, 16:24].unsqueeze(2).to_broadcast([128, 8, 16])),
              reads=[zs_b], writes=[pk_b])
        kb.op(kb.dve, lambda e: e.tensor_copy(out=pk4[:, 3], in_=idx3), reads=[idx_b], writes=[pk_b])
        for q in range(4):
            kb.op(kb.pe, lambda e: e.transpose(ps_t[:, q * 128:(q + 1) * 128], pk[:, q * 128:(q + 1) * 128], identf[:]),
                  reads=[pk_b, identf_b], writes=[ps_t_b])
        kb.op(kb.act, lambda e: e.activation(out=stT[:, :, tt * 128:(tt + 1) * 128], in_=ps_t[:].rearrange("p (q t) -> p q t", q=4),
                                             func=AF.Copy), reads=[ps_t_b], writes=[stT_b])
    kb.end_phase()

    kb.begin_phase()
    iota = kb.sb([128, 128], F32); iota_b = Buf("iota")
    kb.dma(kb.sp, iota[:], iota_d[:, :], writes=[iota_b])
    TC = 32
    scr = [kb.sb([128, TC, 128], F32, f"scr{i}") for i in range(2)]
    scr_b = [Buf(f"scr{i}") for i in range(2)]
    mk = [kb.sb([128, TC, 128], BF16, f"mk{i}") for i in range(2)]
    mk_b = [Buf(f"mk{i}") for i in range(2)]
    cb_ = [kb.sb([128, TC, 128], BF16, f"cb{i}") for i in range(2)]
    cb_b = [Buf(f"cb{i}") for i in range(2)]
    oh = [kb.sb([128, TC, 128], BF16, f"oh{i}") for i in range(2)]
    oh_b = [Buf(f"oh{i}") for i in range(2)]
    Gs = [kb.sb([128, 128, 128], BF16, f"Gs{i}") for i in range(2)]
    Gs_b = [Buf(f"Gs{i}") for i in range(2)]
    pG = [kb.ps([128, 512], F32) for _ in range(4)]
    pG_b = [Buf(f"pG{i}") for i in range(4)]
    kb.drain(kb.sp, [sc_db])
    nev = 0
    kb.drain(kb.pool, [sc_db])

    def load_scr(ch):
        t0 = ch * TC
        p = ch % 2
        for h in range(8):
            kb.dma(kb.sp if h % 2 == 0 else kb.pool, scr[p][h * 16:(h + 1) * 16],
                   sc_d[t0:t0 + TC, h, :].unsqueeze(0).to_broadcast([16, TC, 128]), reads=[sc_db], writes=[scr_b[p]])
    load_scr(0)
    for ch in range(TL // TC):
        t0 = ch * TC
        p = ch % 2
        if ch + 1 < TL // TC:
            load_scr(ch + 1)
        bc = lambda q: stT[:, q, t0:t0 + TC].unsqueeze(2).to_broadcast([128, TC, 128])
        kb.op(kb.dve, lambda e: e.tensor_tensor(out=oh[p][:], in0=iota[:].unsqueeze(1).to_broadcast([128, TC, 128]), in1=bc(3), op=ALU.is_equal),
              reads=[iota_b, stT_b], writes=[oh_b[p]])
        kb.op(kb.dve, lambda e: e.tensor_tensor(out=scr[p][:], in0=scr[p][:], in1=bc(0), op=ALU.add),
              reads=[scr_b[p], stT_b], writes=[scr_b[p]])
        kb.op(kb.dve, lambda e: e.tensor_tensor(out=mk[p][:], in0=scr[p][:], in1=bc(1), op=ALU.is_ge),
              reads=[scr_b[p], stT_b], writes=[mk_b[p]])
        kb.op(kb.dve, lambda e: e.tensor_tensor(out=scr[p][:], in0=scr[p][:], in1=bc(2), op=ALU.subtract),
              reads=[scr_b[p], stT_b], writes=[scr_b[p]])
        kb.op(kb.act, lambda e: e.activation(out=scr[p][:], in_=scr[p][:], func=AF.Exp), reads=[scr_b[p]], writes=[scr_b[p]])
        kb.op(kb.dve, lambda e: e.tensor_tensor(out=cb_[p][:], in0=scr[p][:], in1=mk[p][:], op=ALU.mult),
              reads=[scr_b[p], mk_b[p]], writes=[cb_b[p]])
        gsi = (t0 // 128) % 2
        for t4 in range(TC // 4):
            bk = nev % 4; nev += 1
            for i in range(4):
                t = t4 * 4 + i
                kb.op(kb.pe, lambda e: e.matmul(pG[bk][:, i * 128:(i + 1) * 128], lhsT=oh[p][:, t, :], rhs=cb_[p][:, t, :], start=True, stop=True),
                      reads=[oh_b[p], cb_b[p]], writes=[pG_b[bk]])
            tl = (t0 % 128) + t4 * 4
            kb.op(kb.act, lambda e: e.activation(out=Gs[gsi][:, :, tl:tl + 4].rearrange("p j t -> p t j"),
                                                 in_=pG[bk][:].rearrange("p (t j) -> p t j", t=4), func=AF.Copy),
                  reads=[pG_b[bk]], writes=[Gs_b[gsi]])
        if (t0 + TC) % 128 == 0:
            tb = t0 // 128
            kb.dma(kb.sp, G_d[:, :, tb * 128:(tb + 1) * 128].rearrange("j p t -> p j t"), Gs[gsi][:],
                   reads=[Gs_b[gsi]], writes=[G_db], sembuf=Gs_b[gsi])
    kb.end_phase()

    kb.end_phase()
    act_db = Buf("act_d")
    kb.begin_phase()
    UG = 4
    hnT = kb.sb([128, NDC, TL], BF16, "hnT"); hnT_b = Buf("hnT")
    hnT_src = hnT_d.rearrange("d p t -> p d t")
    for lo in (0, NDC // 2):
        kb.dma(kb.sp, hnT[:, lo:lo + NDC // 2, :], hnT_src[:, lo:lo + NDC // 2, :], reads=[hnT_db], writes=[hnT_b])
    ug = [kb.sb([128, NDC, UG * 128], BF16, f"ug{i}") for i in range(2)]
    gg = [kb.sb([128, UG, TL], BF16, f"gg{i}") for i in range(2)]
    grp_b = [Buf(f"grp{i}") for i in range(2)]
    ge = [kb.sb([128, 512], F32) for _ in range(2)]
    ge_b = [Buf(f"ge{i}") for i in range(2)]
    aT = [kb.sb([128, UG, TL], BF16, f"aT{i}") for i in range(2)]
    aT_b = [Buf(f"aT{i}") for i in range(2)]
    pa = [kb.ps([128, 512], F32) for _ in range(4)]
    pa_b = [Buf(f"pa{i}") for i in range(4)]
    kb.drain(kb.sp, [G_db])
    NG1 = NJ // UG

    def load_grp(g):
        p = g % 2
        for part in range(4):
            kb.dma(kb.pool, ug[p][:, part * 8:(part + 1) * 8, :],
                   uT[g][:, part * 8 * 512:(part + 1) * 8 * 512].rearrange("p (a b) -> p a b", b=512), writes=[grp_b[p]])
        kb.dma(kb.sp, gg[p][:], G_d[g * UG:(g + 1) * UG].rearrange("j p t -> p j t"), reads=[G_db], writes=[grp_b[p]])
    load_grp(0)
    it = 0
    for g in range(NG1):
        if g + 1 < NG1:
            load_grp(g + 1)
        p = g % 2
        for jj in range(UG):
            for th in range(TL // 512):
                q = it % 4; it += 1
                ts = slice(th * 512, (th + 1) * 512)
                for dc in range(NDC):
                    kb.op(kb.pe, lambda e: e.matmul(pa[q][:], lhsT=ug[p][:, dc, jj * 128:(jj + 1) * 128], rhs=hnT[:, dc, ts],
                                                    start=(dc == 0), stop=(dc == NDC - 1)), reads=[grp_b[p], hnT_b], writes=[pa_b[q]])
                kb.op(kb.act, lambda e: e.activation(out=ge[q % 2][:], in_=pa[q][:], func=AF.Gelu), reads=[pa_b[q]], writes=[ge_b[q % 2]])
                kb.op(kb.dve, lambda e: e.tensor_tensor(out=aT[p][:, jj, ts], in0=ge[q % 2][:], in1=gg[p][:, jj, ts], op=ALU.mult),
                      reads=[ge_b[q % 2], grp_b[p]], writes=[aT_b[p]])
        kb.dma(kb.sp, act_d[g * UG:(g + 1) * UG].rearrange("j p t -> p j t"), aT[p][:], reads=[aT_b[p]], writes=[act_db], sembuf=aT_b[p])
    kb.end_phase()

    kb.begin_phase()
    VG = 8
    NBUF = 3
    vg = [kb.sb([128, VG, 512], BF16, f"vg{i}") for i in range(NBUF)]
    ag = [kb.sb([128, VG, TL], BF16, f"ag{i}") for i in range(NBUF)]
    vgrp_b = [Buf(f"vgrp{i}") for i in range(NBUF)]
    agrp_b = [Buf(f"agrp{i}") for i in range(NBUF)]
    po = [kb.ps([128, 512], F32) for _ in range(8)]
    po_b = [Buf(f"po{i}") for i in range(8)]
    hx = [kb.sb([128, 512], F32) for _ in range(4)]
    hx_b = [Buf(f"hx{i}") for i in range(4)]
    kb.drain(kb.sp, [act_db])
    kb.drain(kb.act, [act_db])
    NG2 = NJ // VG
    work = [(cb, jg) for cb in range(D // 512) for jg in range(NG2)]

    def load_v(wi):
        cb, jg = work[wi]
        p = wi % NBUF
        for hf in range(2):
            kb.dma(kb.pool, vg[p][:, hf * 4:(hf + 1) * 4, :],
                   v[cb, jg][:, hf * 2048:(hf + 1) * 2048].rearrange("p (a b) -> p a b", b=512), writes=[vgrp_b[p]])
        kb.dma(kb.act, ag[p][:, 0:4, :], act_d[jg * VG:jg * VG + 4].rearrange("j p t -> p j t"), reads=[act_db], writes=[agrp_b[p]])
        kb.dma(kb.sp, ag[p][:, 4:8, :], act_d[jg * VG + 4:jg * VG + 8].rearrange("j p t -> p j t"), reads=[act_db], writes=[agrp_b[p]])
    for wi in range(min(NBUF - 1, len(work))):
        load_v(wi)
    nh = 0
    for wi, (cb, jg) in enumerate(work):
        if wi + NBUF - 1 < len(work):
            load_v(wi + NBUF - 1)
        p = wi % NBUF
        for tt in range(NTT):
            for jj in range(VG):
                j = jg * VG + jj
                kb.op(kb.pe, lambda e: e.matmul(po[tt][:], lhsT=ag[p][:, jj, tt * 128:(tt + 1) * 128], rhs=vg[p][:, jj, :],
                                                start=(j == 0), stop=(j == NJ - 1)), reads=[vgrp_b[p], agrp_b[p]], writes=[po_b[tt]])
        if jg == NG2 - 1:
            for tt in range(NTT):
                q = nh % 4; nh += 1
                kb.dma(kb.sp, hx[q][:], h_d[tt * 128:(tt + 1) * 128, cb * 512:(cb + 1) * 512], reads=[h_db], writes=[hx_b[q]])
                kb.op(kb.dve, lambda e: e.tensor_tensor(out=hx[q][:], in0=po[tt][:], in1=hx[q][:], op=ALU.add),
                      reads=[po_b[tt], hx_b[q]], writes=[hx_b[q]])
                kb.dma(kb.sp, y[tt * 128:(tt + 1) * 128, cb * 512:(cb + 1) * 512], hx[q][:], reads=[hx_b[q]], writes=[y_db], sembuf=hx_b[q])
    kb.end_phase()


NEED_R = (2, 5, 8)


def front_blocks():
    blocks = []

    def add_heads(c0, nh, kind, gc, hb, need):
        h = 0
        while h < nh:
            n = min(4, nh - h)
            blocks.append((c0 + h * 128, n * 128, kind, gc, hb + h, need))
            h += n
    for g in range(3):
        add_heads(C_KA + g * 6 * 128, 6, "k", 1, g * 6, NEED_R[g])
    add_heads(C_KB, NHB, "k", 3, NHA, 8)
    for g in range(3):
        add_heads(C_VA + g * 6 * 128, 6, "v", None, g * 6, NEED_R[g])
    add_heads(C_VB, NHB, "v", None, NHA, 8)
    blocks.append((C_F, NHB, "f", None, 0, 8))
    nkv = len(blocks)
    add_heads(C_QA, NHA, "q", 0, 0, 0)
    add_heads(C_QB, NHB, "q", 2, NHA, 0)
    return blocks, nkv


def host_wblk(w_in):
    blocks, _ = front_blocks()
    out = np.zeros((len(blocks), 128, NDC * 512), np.float32)
    o4 = out.reshape(len(blocks), 128, NDC, 512)
    for bi, (c0, ncol, *_r) in enumerate(blocks):
        o4[bi, :, :, :ncol] = w_in[:, c0:c0 + ncol].reshape(NDC, 128, ncol).transpose(1, 0, 2)
    return out


def emit_front(kb, xall, x_own, wblk, g1, hg, bfg, qT_d, kall_d, vall_d, lfall_d, out_b):
    blocks, nkv = front_blocks()
    kb.begin_phase()
    ident, ident_b = make_identity(kb, F32)
    ones = kb.sb([128, 128], F32); ones_b = Buf("ones")
    kb.op(kb.pool, lambda e: e.memset(ones[:], 1.0), writes=[ones_b])
    gcol = kb.sb([128, NDC], F32); gcol_b = Buf("gcol")
    kb.dma(kb.sp, gcol[:], g1[:, :], writes=[gcol_b])
    hgc = kb.sb([128, 4], F32); hgc_b = Buf("hgc")
    kb.dma(kb.sp, hgc[:], hg[:, :], writes=[hgc_b])
    for col in (0, 2):
        kb.op(kb.dve, lambda e: e.tensor_scalar_mul(out=hgc[:, col:col + 1], in0=hgc[:, col:col + 1], scalar1=HD ** -0.5),
              reads=[hgc_b], writes=[hgc_b])
    bfs = kb.sb([128, NHB], F32); bfs_b = Buf("bfs")
    kb.dma(kb.sp, bfs[:], bfg[:, :], writes=[bfs_b])
    xnT = kb.sb([128, NDC, TL], BF16, "xnT")
    xnT_bufs = [Buf(f"xnT{t}") for t in range(NTT)]
    xs_tiles = [kb.sb([128, D], F32) for _ in range(2)]
    xs_bufs = [Buf(f"xs{i}") for i in range(2)]
    junk = kb.sb([128, D], BF16); junk_b = Buf("junk")
    stat = [kb.sb([128, 4], F32) for _ in range(2)]
    stat_b = [Buf(f"stat{i}") for i in range(2)]
    psT = [kb.ps([128, 512], F32) for _ in range(2)]
    psT_b = [Buf(f"psT{i}") for i in range(2)]
    wt = [kb.sb([128, NDC, 512], BF16, f"wt{i}") for i in range(2)]
    wt_b = [Buf(f"wt{i}") for i in range(2)]
    pacc = [kb.ps([128, 512], F32) for _ in range(2)]
    pacc_b = [Buf(f"pacc{i}") for i in range(2)]
    pss = [kb.ps([128, 512], F32) for _ in range(2)]
    pss_b = [Buf(f"pss{i}") for i in range(2)]
    sq = [kb.sb([128, 512], F32) for _ in range(2)]
    sq_b = [Buf(f"sq{i}") for i in range(2)]
    rs = [kb.sb([128, 512], F32) for _ in range(2)]
    rs_b = [Buf(f"rs{i}") for i in range(2)]
    stg = [kb.sb([128, 512], BF16) for _ in range(4)]
    stg_b = [Buf(f"stg{i}") for i in range(4)]

    work = []
    for cc in range(NCORES):
        work += [(cc, bi) for bi in range(nkv) if cc < blocks[bi][5]]
        if cc == 0:
            work += [(0, bi) for bi in range(nkv, len(blocks))]

    def load_w(wi):
        bi = work[wi][1]
        for part in range(4):
            kb.dma(kb.pool, wt[wi % 2][:, part * 8:(part + 1) * 8, :],
                   wblk[bi][:, part * 8 * 512:(part + 1) * 8 * 512].rearrange("p (a b) -> p a b", b=512), writes=[wt_b[wi % 2]])

    cnt = {"acc": 0, "ss": 0, "stg": 0}
    load_w(0)
    cur_pass = -1
    staged = ()
    for wi, (ps_, bi) in enumerate(work):
        if ps_ != cur_pass:
            cur_pass = ps_
            src = xall[ps_ * TL:(ps_ + 1) * TL, :]
            emit_norm_transpose(kb, src, gcol, gcol_b, xnT, xnT_bufs, ident, ident_b, psT, psT_b, NTT,
                                xs_tiles, xs_bufs, junk, junk_b, stat, stat_b, staged=staged)
            staged = ()
        if wi + 1 < len(work):
            load_w(wi + 1)
            if work[wi + 1][0] != ps_:
                nsrc = xall[(ps_ + 1) * TL:(ps_ + 2) * TL, :]
                for tt0 in (0, 1):
                    _nt_stage1(kb, nsrc, tt0, xs_tiles, xs_bufs, junk, junk_b, stat, stat_b)
                staged = (0, 1)
        c0, ncol, kind, gc, hb, _need = blocks[bi]
        tok0 = ps_ * TL
        w, wb = wt[wi % 2], wt_b[wi % 2]
        if kind in ("q", "k"):
            for hh in range(ncol // 128):
                for th in range(TL // 512):
                    a = cnt["acc"] % 2; cnt["acc"] += 1
                    pa, pab = pacc[a], pacc_b[a]
                    for dc in range(NDC):
                        kb.op(kb.pe, lambda e: e.matmul(pa[:], lhsT=w[:, dc, hh * 128:(hh + 1) * 128],
                                                        rhs=xnT[:, dc, th * 512:(th + 1) * 512],
                                                        start=(dc == 0), stop=(dc == NDC - 1)),
                              reads=[wb] + xnT_bufs[th * 4:(th + 1) * 4], writes=[pab])
                    s_ = cnt["ss"] % 2; cnt["ss"] += 1
                    kb.op(kb.act, lambda e: e.activation(out=sq[s_][:], in_=pa[:], func=AF.Square), reads=[pab], writes=[sq_b[s_]])
                    kb.op(kb.pe, lambda e: e.matmul(pss[s_][:], lhsT=ones[:], rhs=sq[s_][:], start=True, stop=True),
                          reads=[ones_b, sq_b[s_]], writes=[pss_b[s_]])
                    kb.op(kb.act, lambda e: e.activation(out=rs[s_][:], in_=pss[s_][:], func=AF.Sqrt, scale=1.0 / HD, bias=EPS),
                          reads=[pss_b[s_]], writes=[rs_b[s_]])
                    kb.op(kb.dve, lambda e: e.reciprocal(out=rs[s_][:], in_=rs[s_][:]), reads=[rs_b[s_]], writes=[rs_b[s_]])
                    g_ = cnt["stg"] % 4; cnt["stg"] += 1
                    kb.op(kb.dve, lambda e: e.scalar_tensor_tensor(out=stg[g_][:], in0=pa[:], scalar=hgc[:, gc:gc + 1],
                                                                   in1=rs[s_][:], op0=ALU.mult, op1=ALU.mult),
                          reads=[pab, hgc_b, rs_b[s_]], writes=[stg_b[g_]])
                    if kind == "q":
                        dst = qT_d[hb + hh, :, th * 512:(th + 1) * 512]
                    else:
                        dst = kall_d[hb + hh, :, tok0 + th * 512:tok0 + (th + 1) * 512]
                    kb.dma(kb.sp, dst, stg[g_][:], reads=[stg_b[g_]], writes=[out_b], sembuf=stg_b[g_])
        elif kind == "v":
            nh = ncol // 128
            for tt in range(NTT):
                a = cnt["acc"] % 2; cnt["acc"] += 1
                pa, pab = pacc[a], pacc_b[a]
                for dc in range(NDC):
                    kb.op(kb.pe, lambda e: e.matmul(pa[:, 0:ncol], lhsT=xnT[:, dc, tt * 128:(tt + 1) * 128],
                                                    rhs=w[:, dc, 0:ncol], start=(dc == 0), stop=(dc == NDC - 1)),
                          reads=[wb, xnT_bufs[tt]], writes=[pab])
                g_ = cnt["stg"] % 4; cnt["stg"] += 1
                kb.op(kb.act, lambda e: e.activation(out=stg[g_][:, 0:ncol], in_=pa[:, 0:ncol], func=AF.Copy),
                      reads=[pab], writes=[stg_b[g_]])
                kb.dma(kb.sp, vall_d[hb:hb + nh, :, ps_ * NTT + tt, :].rearrange("h k d -> k h d"),
                       stg[g_][:, 0:ncol].rearrange("t (h d) -> t h d", h=nh),
                       reads=[stg_b[g_]], writes=[out_b], sembuf=stg_b[g_])
        else:
            for tt in range(NTT):
                a = cnt["acc"] % 2; cnt["acc"] += 1
                pa, pab = pacc[a], pacc_b[a]
                for dc in range(NDC):
                    kb.op(kb.pe, lambda e: e.matmul(pa[:, 0:NHB], lhsT=xnT[:, dc, tt * 128:(tt + 1) * 128],
                                                    rhs=w[:, dc, 0:NHB], start=(dc == 0), stop=(dc == NDC - 1)),
                          reads=[wb, xnT_bufs[tt]], writes=[pab])
                s_ = cnt["ss"] % 2; cnt["ss"] += 1
                z = sq[s_]; zb = sq_b[s_]
                kb.op(kb.dve, lambda e: e.tensor_tensor(out=z[:, 0:NHB], in0=pa[:, 0:NHB], in1=bfs[:], op=ALU.add),
                      reads=[pab, bfs_b], writes=[zb])
                kb.op(kb.act, lambda e: e.activation(out=z[:, 0:NHB], in_=z[:, 0:NHB], func=AF.Exp, scale=-1.0), reads=[zb], writes=[zb])
                kb.op(kb.act, lambda e: e.activation(out=z[:, 0:NHB], in_=z[:, 0:NHB], func=AF.Ln, bias=1.0), reads=[zb], writes=[zb])
                kb.op(kb.dve, lambda e: e.tensor_scalar_mul(out=z[:, 16:16 + NHB], in0=z[:, 0:NHB], scalar1=-1.0), reads=[zb], writes=[zb])
                kb.dma(kb.sp, lfall_d[:, ps_ * NTT + tt, :], z[:, 16:16 + NHB],
                       reads=[zb], writes=[out_b], sembuf=zb)
    kb.end_phase()


def build_fused():
    nc = bass.Bass("TRN2", target_bir_lowering=False)
    ext = lambda n, sh, dt: nc.dram_tensor(n, sh, dt, kind="ExternalInput").ap()
    blocks, _ = front_blocks()
    xall = ext("xall", [S, D], F32)
    x = ext("x", [TL, D], F32)
    wblk = ext("wblk", [len(blocks), 128, NDC * 512], F32)
    hg = ext("hg", [128, 4], F32)
    bfg = ext("bfg", [128, NHB], F32)
    cst = {n: ext(n, sh, dt) for n, sh, dt in CONST_SPECS}
    g1 = ext("g1", [128, NDC], F32)
    g2 = ext("g2", [128, NDC], F32)
    w_g = ext("w_g", [16, 128, 2 * NDC * 256], F32)
    w_up_a = ext("w_up_a", [768, D], F32)
    w_up_b = ext("w_up_b", [WB, D], F32)
    w_out = ext("w_out", [D, D], F32)
    w_pq = ext("w_pq", [D, PH * 2 * QH], F32)
    skT = ext("skT", [16, 128, 128], F32)
    uT = ext("uT", [32, 128, NDC * 512], F32)
    v = ext("v", [8, 16, 128, 8 * 512], F32)
    iota_d = ext("iota", [128, 128], F32)
    qT_d = nc.dram_tensor("qT_d", [NH, 128, TL], BF16).ap()
    kall_d = nc.dram_tensor("kall_d", [NH, 128, S], BF16).ap()
    vall_d = nc.dram_tensor("vall_d", [NH, 128, S // 128, 128], BF16).ap()
    lfall_d = nc.dram_tensor("lfall_d", [128, S // 128, NHB], F32).ap()
    yT_d = nc.dram_tensor("yT_d", [20, 128, TL], BF16).ap()
    merged_d = nc.dram_tensor("merged_d", [NDC, 128, TL], BF16).ap()
    h_d = nc.dram_tensor("h_d", [TL, D], F32).ap()
    hnT_d = nc.dram_tensor("hnT_d", [NDC, 128, TL], BF16).ap()
    sc_d = nc.dram_tensor("sc_d", [TL, 8, 128], F32).ap()
    G_d = nc.dram_tensor("G_d", [128, 128, TL], BF16).ap()
    act_d = nc.dram_tensor("act_d", [128, 128, TL], BF16).ap()
    y = nc.dram_tensor("y", [TL, D], F32, kind="ExternalOutput").ap()
    with ExitStack() as es:
        kb = KB(nc, es)
        yT_db, h_db, hnT_db, y_db, fr_db = Buf("yT_d"), Buf("h_d"), Buf("hnT_d"), Buf("y_d"), Buf("front")
        emit_front(kb, xall, x, wblk, g1, hg, bfg, qT_d, kall_d, vall_d, lfall_d, fr_db)
        kb.begin_phase()
        emit_attention(kb, qT_d, kall_d, vall_d, lfall_d, cst, yT_d, yT_db)
        kb.end_phase()
        emit_mid(kb, x, g1, w_g, w_up_a, w_up_b, w_out, g2, yT_d, yT_db, merged_d, h_d, h_db, hnT_d, hnT_db)
        emit_peer(kb, hnT_d, hnT_db, h_d, h_db, w_pq, skT, uT, v, iota_d, sc_d, G_d, act_d, y, y_db)
        kb.drain(kb.sp, [y_db])
    return nc


def kernel_fused(**inp):
    f32 = lambda a: np.asarray(a, np.float32)
    x = f32(inp["x"]).reshape(S, D)
    xb = x.reshape(S // 128, 128, D)
    xs = [np.ascontiguousarray(xb[c::NCORES].reshape(TL, D)) for c in range(NCORES)]
    w_in = f32(inp["w_in"])[0]
    wblk = host_wblk(w_in)
    w_g = np.ascontiguousarray(np.stack(
        [w_in[:, c0:c0 + D].reshape(NDC, 128, 16, 256).transpose(2, 1, 0, 3) for c0 in (C_GA, C_GB)], axis=2).reshape(16, 128, 2 * NDC * 256))
    g1 = np.ascontiguousarray(f32(inp["norm1_gain"])[0].reshape(NDC, 128).T)
    g2 = np.ascontiguousarray(f32(inp["norm2_gain"])[0].reshape(NDC, 128).T)
    hg = np.ascontiguousarray(np.stack([f32(inp[k])[0] for k in ("q_norm_a", "k_norm_a", "q_norm_b", "k_norm_b")], axis=1))
    bfg = np.ascontiguousarray(np.broadcast_to(f32(inp["b_forget"])[0][None, :], (128, NHB)))
    u3 = f32(inp["peer_u"])[0].reshape(128, 128, D)
    uT = np.ascontiguousarray(u3.reshape(128, 32, 4, NDC, 128).transpose(1, 4, 3, 2, 0).reshape(32, 128, NDC * 512))
    v = np.ascontiguousarray(f32(inp["peer_v"])[0].reshape(128, 16, 8, 8, 512).transpose(3, 1, 0, 2, 4).reshape(8, 16, 128, 8 * 512))
    skT = np.ascontiguousarray(f32(inp["peer_subkeys"])[0].reshape(16, 128, 128).transpose(0, 2, 1))
    iota = np.ascontiguousarray(np.broadcast_to(np.arange(128, dtype=np.float32)[None, :], (128, 128)))
    common = {"wblk": wblk, "hg": hg, "bfg": bfg, "g1": g1, "g2": g2, "w_g": w_g,
              "w_up_a": np.ascontiguousarray(f32(inp["w_up_a"])[0]), "w_up_b": np.ascontiguousarray(f32(inp["w_up_b"])[0]),
              "w_out": np.ascontiguousarray(f32(inp["w_out"])[0]), "w_pq": np.ascontiguousarray(f32(inp["w_peer_q"])[0]),
              "skT": skT, "uT": uT, "v": v, "iota": iota}
    in_maps = []
    for c in range(NCORES):
        m = dict(common)
        m["x"] = xs[c]
        m["xall"] = np.ascontiguousarray(np.concatenate([xs[(c - r) % NCORES] for r in range(NCORES)], axis=0))
        m.update(core_consts(c))
        in_maps.append(m)
    nc = build_fused()
    r = run_bass_kernel_spmd(nc, in_maps, core_ids=list(range(NCORES))).results
    out = np.empty((S // 128, 128, D), np.float32)
    for c in range(NCORES):
        out[c::NCORES] = np.asarray(r[c]["y"], np.float32).reshape(NTT, 128, D)
    return out.reshape(1, S, D)


kernel = kernel_fused
```

```python
import math
from contextlib import ExitStack

import numpy as np
import ml_dtypes

import concourse.bass as bass
import concourse.mybir as mybir
from concourse.bass_utils import run_bass_kernel_spmd

F32 = mybir.dt.float32
BF16 = mybir.dt.bfloat16
U32 = mybir.dt.uint32
AF = mybir.ActivationFunctionType
ALU = mybir.AluOpType
AX = mybir.AxisListType

NCORES = 8
D = 4096
S = 8192
TL = S // NCORES
NTT = TL // 128
NDC = D // 128
HD = 128
NHA, NHB = 18, 14
NH = NHA + NHB
WA, WB = NHA * HD, NHB * HD
IN_COLS = 3 * WA + 3 * WB + NHB + 2 * D
C_QA, C_KA, C_VA = 0, WA, 2 * WA
C_QB, C_KB, C_VB = 3 * WA, 3 * WA + WB, 3 * WA + 2 * WB
C_F = 3 * WA + 3 * WB
C_GA = C_F + NHB
C_GB = C_GA + D
EPS = 1e-6
NEG = -1e30
GROUPS = ((128, 1), (512, 4), (2048, 16))
PH, NK, TOPK, QH = 8, 128, 16, 128


class Buf:
    __slots__ = ("w", "r", "dsem", "dcnt", "name")

    def __init__(self, name=""):
        self.w = {}
        self.r = {}
        self.dsem = None
        self.dcnt = 0
        self.name = name


class Eng:
    def __init__(self, kb, name, h):
        self.h = h
        self.name = name
        self.sem = kb.new_sem("e_" + name)
        self.n = 0
        self.known = {}


class KB:
    def __init__(self, nc, es):
        self.nc = nc
        self.es = es
        self.nsem = 0
        self.pe = Eng(self, "pe", nc.tensor)
        self.act = Eng(self, "act", nc.scalar)
        self.dve = Eng(self, "dve", nc.vector)
        self.pool = Eng(self, "pool", nc.gpsimd)
        self.sp = Eng(self, "sp", nc.sync)
        self.uid = 0
        self.engs = [self.pe, self.act, self.dve, self.pool, self.sp]
        self.free_dsems = []
        self.live_dbufs = []
        self.phases = []

    @property
    def phase_es(self):
        return self.phases[-1][0] if self.phases else None

    def begin_phase(self):
        es = ExitStack()
        es.__enter__()
        self.phases.append((es, self.live_dbufs))
        self.live_dbufs = []

    def end_phase(self):
        for e in self.engs:
            for e2 in self.engs:
                if e2 is e or e2.n == 0:
                    continue
                k = id(e2.sem)
                if e.known.get(k, 0) < e2.n:
                    e.h.wait_ge(e2.sem, e2.n)
                    e.known[k] = e2.n
            allb = list(self.live_dbufs)
            for _, lst in self.phases:
                allb += lst
            for b in allb:
                k = id(b.dsem)
                if e.known.get(k, 0) < b.dcnt:
                    e.h.wait_ge(b.dsem, b.dcnt)
                    e.known[k] = b.dcnt
        for b in self.live_dbufs:
            self.free_dsems.append((b.dsem, b.dcnt))
            b.dsem = None
        es, outer = self.phases.pop()
        self.live_dbufs = outer
        es.__exit__(None, None, None)

    def new_sem(self, name):
        self.nsem += 1
        return self.es.enter_context(self.nc.semaphore(name))

    def sb(self, shape, dt, name=None):
        self.uid += 1
        return (self.phase_es or self.es).enter_context(self.nc.sbuf_tensor(f"{name or 'sb'}_{self.uid}", list(shape), dt))

    def ps(self, shape, dt, name=None):
        self.uid += 1
        return (self.phase_es or self.es).enter_context(self.nc.psum_tensor(f"{name or 'ps'}_{self.uid}", list(shape), dt))

    def _deps(self, reads, writes):
        deps = {}
        for b in reads:
            for k, (s, v) in b.w.items():
                if k not in deps or deps[k][1] < v:
                    deps[k] = (s, v)
        for b in writes:
            for dct in (b.w, b.r):
                for k, (s, v) in dct.items():
                    if k not in deps or deps[k][1] < v:
                        deps[k] = (s, v)
        return deps

    def _wait(self, eng, deps):
        for k, (s, v) in deps.items():
            if s is eng.sem and eng is self.pe:
                continue
            if eng.known.get(k, 0) >= v:
                continue
            eng.h.wait_ge(s, v)
            eng.known[k] = v

    def op(self, eng, fn, reads=(), writes=()):
        self._wait(eng, self._deps(reads, writes))
        ins = fn(eng.h)
        eng.n += 1
        ins.then_inc(eng.sem, 1)
        k = id(eng.sem)
        tok = (eng.sem, eng.n)
        eng.known[k] = max(eng.known.get(k, 0), 0)
        for b in writes:
            b.w[k] = tok
        for b in reads:
            b.r[k] = tok
        return ins

    def dma(self, q, out, in_, reads=(), writes=(), sembuf=None, **kw):
        self._wait(q, self._deps(reads, writes))
        sb_ = sembuf or (writes[0] if writes else reads[0])
        if sb_.dsem is None:
            if self.free_dsems:
                sb_.dsem, sb_.dcnt = self.free_dsems.pop()
            else:
                sb_.dsem = self.new_sem(f"d{self.nsem}")
                sb_.dcnt = 0
            self.live_dbufs.append(sb_)
        ins = q.h.dma_start(out=out, in_=in_, **kw)
        ins.then_inc(sb_.dsem, 16)
        sb_.dcnt += 16
        k = id(sb_.dsem)
        tok = (sb_.dsem, sb_.dcnt)
        for b in writes:
            b.w[k] = tok
        for b in reads:
            b.r[k] = tok
        return ins

    def drain(self, eng, bufs):
        self._wait(eng, self._deps(bufs, ()))


def make_identity(kb, dt):
    ident = kb.sb([128, 128], dt)
    b = Buf("ident")
    kb.op(kb.pool, lambda e: e.memset(ident[:], 0.0), writes=[b])
    kb.op(kb.pool, lambda e: e.affine_select(out=ident[:], in_=ident[:], pattern=[[-1, 128]],
                                             compare_op=ALU.not_equal, fill=1.0, base=0,
                                             channel_multiplier=1), reads=[b], writes=[b])
    return ident, b


def _nt_stage1(kb, x_ap, tt, xs_tiles, xs_bufs, junk, junk_b, stat, stat_b):
    xs, xb = xs_tiles[tt % 2], xs_bufs[tt % 2]
    kb.dma(kb.sp, xs[:], x_ap[tt * 128:(tt + 1) * 128, :], writes=[xb])
    st, sb_ = stat[tt % 2], stat_b[tt % 2]
    kb.op(kb.act, lambda e: e.activation(out=junk[:], in_=xs[:], func=AF.Square, accum_out=st[:, 0:1]),
          reads=[xb], writes=[junk_b, sb_])
    kb.op(kb.act, lambda e: e.activation(out=st[:, 1:2], in_=st[:, 0:1], func=AF.Sqrt, scale=1.0 / D, bias=EPS),
          reads=[sb_], writes=[sb_])
    kb.op(kb.dve, lambda e: e.reciprocal(out=st[:, 2:3], in_=st[:, 1:2]), reads=[sb_], writes=[sb_])
    kb.op(kb.act, lambda e: e.activation(out=xs[:], in_=xs[:], func=AF.Copy, scale=st[:, 2:3]),
          reads=[xb, sb_], writes=[xb])


def emit_norm_transpose(kb, x_ap, gcol, gcol_b, xnT, xnT_bufs, ident, ident_b, psT, psT_b, ntt,
                        xs_tiles, xs_bufs, junk, junk_b, stat, stat_b, staged=()):
    for tt in range(ntt):
        xs, xb = xs_tiles[tt % 2], xs_bufs[tt % 2]
        if tt not in staged:
            _nt_stage1(kb, x_ap, tt, xs_tiles, xs_bufs, junk, junk_b, stat, stat_b)
        for g in range(NDC // 4):
            p, pb = psT[g % 2], psT_b[g % 2]
            for i in range(4):
                dc = g * 4 + i
                kb.op(kb.pe, lambda e: e.transpose(p[:, i * 128:(i + 1) * 128],
                                                   xs[:, dc * 128:(dc + 1) * 128], ident[:]),
                      reads=[xb, ident_b], writes=[pb])
            kb.op(kb.dve, lambda e: e.tensor_tensor(
                out=xnT[:, g * 4:(g + 1) * 4, tt * 128:(tt + 1) * 128],
                in0=p[:].rearrange("p (a b) -> p a b", a=4),
                in1=gcol[:, g * 4:(g + 1) * 4].unsqueeze(2).to_broadcast([128, 4, 128]),
                op=ALU.mult), reads=[pb, gcol_b], writes=[xnT_bufs[tt]])


def alibi_slope(h):
    return 2.0 ** (-8.0 * (h + 1) / NHA)


def emit_attention(kb, qT, kall, vall, lfall, cst, yT_d, yT_db, heads_a=range(6), heads_b=range(NHB)):
    nc = kb.nc
    identb, identb_b = make_identity(kb, BF16)
    identf, identf_b = make_identity(kb, F32)
    def load_const(name, shape, dt, q=None):
        t = kb.sb(shape, dt, "c_" + name)
        b = Buf("c_" + name)
        kb.dma(q or kb.sp, t[:], cst[name], writes=[b])
        return t, b
    fmask, fmask_b = load_const("fmask", [128, 8 * 128], BF16)
    ndist, ndist_b = load_const("ndist", [128, 24 * 128], F32)
    vms = [load_const(f"vm{g}", [128, 24 * 128], F32) for g in range(3)]
    LT, LT_b = load_const("LT", [64, 8], F32)
    LTF, LTF_b = load_const("LTF", [64, 64], F32)
    i8e, i8e_b = load_const("i8e", [8, TL], BF16)
    ones8 = kb.sb([8, 128], BF16); ones8_b = Buf("ones8")
    kb.op(kb.pool, lambda e: e.memset(ones8[:], 1.0), writes=[ones8_b])
    onesf = kb.sb([128, 128], F32); onesf_b = Buf("onesf")
    kb.op(kb.pool, lambda e: e.memset(onesf[:], 1.0), writes=[onesf_b])
    zer = kb.sb([128, 512], BF16); zer_b = Buf("zer")
    kb.op(kb.pool, lambda e: e.memset(zer[:], 0.0), writes=[zer_b])
    tri = kb.sb([128, 128], F32); tri_b = Buf("tri")
    kb.op(kb.pool, lambda e: e.memset(tri[:], 1.0), writes=[tri_b])
    kb.op(kb.pool, lambda e: e.affine_select(out=tri[:], in_=tri[:], pattern=[[1, 128]], compare_op=ALU.is_ge,
                                             fill=0.0, base=0, channel_multiplier=-1), reads=[tri_b], writes=[tri_b])

    banks = [kb.ps([128, 512], F32, f"bank{i}") for i in range(7)]
    bank_b = [Buf(f"bank{i}") for i in range(7)]
    pst = kb.ps([128, 1024], BF16, "pst"); pst_b = Buf("pst")

    NB = S // 128
    LF = kb.sb([128, NB * NHB], F32); LF_b = Buf("LF")
    kb.dma(kb.sp, LF[:], lfall.rearrange("k t h -> k (t h)"), writes=[LF_b])
    HALF = NB * NHB // 2
    for hf in range(2):
        kb.op(kb.pe, lambda e: e.matmul(banks[hf][:, 0:HALF], lhsT=tri[:], rhs=LF[:, hf * HALF:(hf + 1) * HALF],
                                        start=True, stop=True), reads=[tri_b, LF_b], writes=[bank_b[hf]])
    call = kb.sb([128, NB * NHB], F32); call_b = Buf("call")
    srep = kb.sb([128, NB * NHB], F32); srep_b = Buf("srep")
    off = kb.sb([128, NB * NHB], F32); off_b = Buf("off")
    LF3 = LF[:].rearrange("p (b h) -> p b h", h=NHB)
    for h in range(NHB):
        kb.op(kb.pe, lambda e: e.matmul(banks[4][0:NB, h:h + 1], lhsT=LF3[:, :, h], rhs=onesf[:, 0:1], start=True, stop=True),
              reads=[LF_b, onesf_b], writes=[bank_b[4]])
    totT = kb.sb([NB, NHB], F32); totT_b = Buf("totT")
    kb.op(kb.act, lambda e: e.activation(out=totT[:], in_=banks[4][0:NB, 0:NHB], func=AF.Copy), reads=[bank_b[4]], writes=[totT_b])
    kb.op(kb.pe, lambda e: e.matmul(banks[6][0:NB, 0:NHB], lhsT=LTF[:], rhs=totT[:], start=True, stop=True),
          reads=[LTF_b, totT_b], writes=[bank_b[6]])
    offT = kb.sb([NB, NHB], F32); offT_b = Buf("offT")
    kb.op(kb.act, lambda e: e.activation(out=offT[:], in_=banks[6][0:NB, 0:NHB], func=AF.Copy), reads=[bank_b[6]], writes=[offT_b])
    Rm = kb.sb([NB, NB * NHB], F32); Rm_b = Buf("Rm")
    kb.op(kb.dve, lambda e: e.tensor_tensor(out=Rm[:].rearrange("p (b h) -> p b h", h=NHB),
                                            in0=identf[0:NB, 0:NB].unsqueeze(2).to_broadcast([NB, NB, NHB]),
                                            in1=offT[:].unsqueeze(1).to_broadcast([NB, NB, NHB]), op=ALU.mult),
          reads=[identf_b, offT_b], writes=[Rm_b])
    for hf in range(2):
        kb.op(kb.pe, lambda e: e.matmul(banks[2 + hf][:, 0:HALF], lhsT=onesf[0:NB, :], rhs=Rm[:, hf * HALF:(hf + 1) * HALF],
                                        start=True, stop=True), reads=[onesf_b, Rm_b], writes=[bank_b[2 + hf]])
        kb.op(kb.act, lambda e: e.activation(out=off[:, hf * HALF:(hf + 1) * HALF], in_=banks[2 + hf][:, 0:HALF], func=AF.Copy),
              reads=[bank_b[2 + hf]], writes=[off_b])
    for hf in range(2):
        kb.op(kb.dve, lambda e: e.tensor_tensor(out=call[:, hf * HALF:(hf + 1) * HALF], in0=banks[hf][:, 0:HALF],
                                                in1=off[:, hf * HALF:(hf + 1) * HALF], op=ALU.add),
              reads=[bank_b[hf], off_b], writes=[call_b])
    negc = kb.sb([128, NB * NHB], F32); negc_b = Buf("negc")
    kb.op(kb.dve, lambda e: e.tensor_scalar_mul(out=negc[:], in0=call[:], scalar1=-1.0), reads=[call_b], writes=[negc_b])
    negc_v = negc[:].rearrange("p (b h) -> p b h", h=NHB)
    kb.op(kb.pe, lambda e: e.matmul(banks[5][0:8, 0:NHB], lhsT=LT[:], rhs=totT[:], start=True, stop=True),
          reads=[LT_b, totT_b], writes=[bank_b[5]])
    offo = kb.sb([8, NHB], F32); offo_b = Buf("offo")
    kb.op(kb.act, lambda e: e.activation(out=offo[:], in_=banks[5][0:8, 0:NHB], func=AF.Copy), reads=[bank_b[5]], writes=[offo_b])
    crow = kb.sb([8, NHB * TL], BF16); crow_b = Buf("crow")
    for h in range(NHB):
        kb.op(kb.dve, lambda e: e.tensor_scalar_mul(out=crow[:, h * TL:(h + 1) * TL], in0=i8e[:], scalar1=offo[:, h:h + 1]),
              reads=[i8e_b, offo_b], writes=[crow_b])

    KT = [kb.sb([128, S], BF16, f"KT{i}") for i in range(2)]
    KT_b = [Buf(f"KT{i}") for i in range(2)]
    VA = [kb.sb([128, NB * 129], BF16, f"VA{i}") for i in range(2)]
    VA_b = [Buf(f"VA{i}") for i in range(2)]
    QT = [kb.sb([128, TL], BF16, f"QT{i}") for i in range(2)]
    QT_b = [Buf(f"QT{i}") for i in range(2)]
    for i in range(2):
        kb.op(kb.pool, lambda e: e.memset(VA[i][:], 1.0), writes=[VA_b[i]])
    PT = [kb.sb([128, 512], BF16, f"PT{i}") for i in range(2)]
    PT_b = [Buf(f"PT{i}") for i in range(2)]
    LG = [kb.sb([128, 512], F32, f"LG{i}") for i in range(2)]
    LG_b = [Buf(f"LG{i}") for i in range(2)]
    biash = kb.sb([128, 24 * 128], F32); biash_b = Buf("biash")
    ytok = [kb.sb([128, 128], BF16) for _ in range(2)]
    ytok_b = [Buf(f"ytok{i}") for i in range(2)]
    rden = [kb.sb([128, 1], F32) for _ in range(2)]
    rden_b = [Buf(f"rden{i}") for i in range(2)]
    yT = [kb.sb([128, TL], BF16, f"yT{i}") for i in range(2)]
    yT_b = [Buf(f"yTs{i}") for i in range(2)]

    seq = []
    for s_ in heads_a:
        for g in range(3):
            seq.append(("a", s_, g, 6 * g + s_))
    for h in heads_b:
        seq.append(("b", 6 + h, h, NHA + h))

    def load_head(i):
        kind, slot, g, kh = seq[i]
        p = i % 2
        nr = NEED_R[g] if kind == "a" else 8
        kb.dma(kb.sp, KT[p][:, 0:nr * TL], kall[kh][:, 0:nr * TL], writes=[KT_b[p]])
        kb.dma(kb.sp, VA[p][:].rearrange("p (t e) -> p t e", e=129)[:, 0:nr * NTT, 0:128],
               vall[kh][:, 0:nr * NTT, :], writes=[VA_b[p]])
        kb.dma(kb.sp, QT[p][:], qT[kh], writes=[QT_b[p]])

    cnt = {"u": 0, "y": 0, "yt": 0}

    def finish(acc_ap, acc_buf, slot_par, j):
        y = cnt["y"] % 2; cnt["y"] += 1
        kb.op(kb.dve, lambda e: e.reciprocal(out=rden[y][:], in_=acc_ap[:, 128:129]), reads=[acc_buf], writes=[rden_b[y]])
        kb.op(kb.dve, lambda e: e.tensor_scalar_mul(out=ytok[y][:], in0=acc_ap[:, 0:128], scalar1=rden[y][:, 0:1]),
              reads=[acc_buf, rden_b[y]], writes=[ytok_b[y]])
        kb.op(kb.pe, lambda e: e.transpose(pst[:, j * 128:(j + 1) * 128], ytok[y][:], identb[:]),
              reads=[ytok_b[y], identb_b], writes=[pst_b])
        kb.op(kb.act, lambda e: e.activation(out=yT[slot_par][:, j * 128:(j + 1) * 128], in_=pst[:, j * 128:(j + 1) * 128], func=AF.Copy),
              reads=[pst_b], writes=[yT_b[slot_par]])

    if seq:
        load_head(0)
    nslot = 0
    for i, (kind, slot, g, kh) in enumerate(seq):
        if i + 1 < len(seq):
            load_head(i + 1)
        p = i % 2
        kt, ktb, va, vab, qt, qtb = KT[p], KT_b[p], VA[p], VA_b[p], QT[p], QT_b[p]
        va3 = va[:].rearrange("p (t e) -> p t e", e=129)
        if kind == "a":
            accs = [banks[2 + j // 3][:, (j % 3) * 129:(j % 3) * 129 + 129] for j in range(8)]
            accs_b = [bank_b[2 + j // 3] for j in range(8)]
            if g == 0:
                for bk in range(3):
                    kb.op(kb.pe, lambda e: e.matmul(banks[2 + bk][:], lhsT=zer[:, 0:128], rhs=zer[:], start=True, stop=True),
                          reads=[zer_b], writes=[bank_b[2 + bk]])
            ndm = (2, 2, 3)[g]
            ntile = ndm * 8
            vm, vmb = vms[g]
            kb.op(kb.dve, lambda e: e.scalar_tensor_tensor(out=biash[:, 0:ntile * 128], in0=ndist[:, 0:ntile * 128],
                                                           scalar=float(alibi_slope(kh)), in1=vm[:, 0:ntile * 128],
                                                           op0=ALU.mult, op1=ALU.add),
                  reads=[ndist_b, vmb], writes=[biash_b])
            units = []
            for j in range(8):
                for dm in range(ndm):
                    if j - dm < 0:
                        continue
                    for k0 in range(0, NEED_R[g], 4):
                        units.append((j, [(dm, c_) for c_ in range(k0, min(k0 + 4, NEED_R[g]))]))

            def qk(u):
                j, tl = units[u]
                sb_i = cnt["u"] % 2
                for ii, (dm, c_) in enumerate(tl):
                    col = (c_ * 8 + (j - dm)) * 128
                    kb.op(kb.pe, lambda e: e.matmul(banks[sb_i][:, ii * 128:(ii + 1) * 128], lhsT=kt[:, col:col + 128],
                                                    rhs=qt[:, j * 128:(j + 1) * 128], start=True, stop=True),
                          reads=[ktb, qtb], writes=[bank_b[sb_i]])
                n = len(tl) * 128
                b0 = (tl[0][0] * 8 + tl[0][1]) * 128
                kb.op(kb.dve, lambda e: e.tensor_tensor(out=LG[sb_i][:, 0:n], in0=banks[sb_i][:, 0:n], in1=biash[:, b0:b0 + n], op=ALU.add),
                      reads=[bank_b[sb_i], biash_b], writes=[LG_b[sb_i]])
                kb.op(kb.act, lambda e: e.activation(out=PT[sb_i][:, 0:n], in_=LG[sb_i][:, 0:n], func=AF.Exp),
                      reads=[LG_b[sb_i]], writes=[PT_b[sb_i]])
                cnt["u"] += 1
                return sb_i

            def pv(u, sb_i):
                j, tl = units[u]
                for ii, (dm, c_) in enumerate(tl):
                    t_ = c_ * 8 + (j - dm)
                    kb.op(kb.pe, lambda e: e.matmul(accs[j], lhsT=PT[sb_i][:, ii * 128:(ii + 1) * 128], rhs=va3[:, t_, :],
                                                    start=False, stop=False),
                          reads=[PT_b[sb_i], vab], writes=[accs_b[j]])
            prev = qk(0)
            for u in range(len(units)):
                nxt = qk(u + 1) if u + 1 < len(units) else None
                pv(u, prev)
                prev = nxt
            if g == 2:
                sp_ = nslot % 2; nslot += 1
                for j in range(8):
                    finish(accs[j], accs_b[j], sp_, j)
                kb.dma(kb.sp, yT_d[slot], yT[sp_][:], reads=[yT_b[sp_]], writes=[yT_db], sembuf=yT_b[sp_])
        else:
            h = g
            sp_ = nslot % 2; nslot += 1
            for G in range(2):
                i0 = 4 * G
                accs = [banks[2 + a][:, 0:129] for a in range(4)]
                accs_b = [bank_b[2 + a] for a in range(4)]
                tiles = [(m, c_) for m in range(i0 + 4) for c_ in range(8)]

                def qk(t):
                    m, c_ = tiles[t]
                    sb_i = cnt["u"] % 2
                    ilo = max(m, i0)
                    c0, c1 = (ilo - i0) * 128, 512
                    col = (c_ * 8 + m) * 128
                    last_is_mask = m >= i0
                    kb.op(kb.pe, lambda e: e.matmul(banks[sb_i][:, c0:c1], lhsT=kt[:, col:col + 128],
                                                    rhs=qt[:, ilo * 128:(i0 + 4) * 128], start=True, stop=False),
                          reads=[ktb, qtb], writes=[bank_b[sb_i]])
                    kb.op(kb.pe, lambda e: e.matmul(banks[sb_i][:, c0:c1], lhsT=ones8[:],
                                                    rhs=crow[:, h * TL + ilo * 128:h * TL + (i0 + 4) * 128],
                                                    start=False, stop=not last_is_mask),
                          reads=[ones8_b, crow_b], writes=[bank_b[sb_i]])
                    if last_is_mask:
                        kb.op(kb.pe, lambda e: e.matmul(banks[sb_i][:, c0:c0 + 128], lhsT=identb[:],
                                                        rhs=fmask[:, c_ * 128:(c_ + 1) * 128], start=False, stop=True),
                              reads=[identb_b, fmask_b], writes=[bank_b[sb_i]])
                    kb.op(kb.act, lambda e: e.activation(out=PT[sb_i][:, c0:c1], in_=banks[sb_i][:, c0:c1], func=AF.Exp,
                                                         bias=negc_v[:, c_ * 8 + m, h:h + 1]),
                          reads=[bank_b[sb_i], negc_b], writes=[PT_b[sb_i]])
                    cnt["u"] += 1
                    return sb_i

                def pv(t, sb_i):
                    m, c_ = tiles[t]
                    t_ = c_ * 8 + m
                    for ii in range(max(m, i0), i0 + 4):
                        a = ii - i0
                        kb.op(kb.pe, lambda e: e.matmul(accs[a], lhsT=PT[sb_i][:, a * 128:(a + 1) * 128], rhs=va3[:, t_, :],
                                                        start=(t == 0), stop=(m == ii and c_ == 7)),
                              reads=[PT_b[sb_i], vab], writes=[accs_b[a]])
                prev = qk(0)
                for t in range(len(tiles)):
                    nxt = qk(t + 1) if t + 1 < len(tiles) else None
                    pv(t, prev)
                    prev = nxt
                for a in range(4):
                    finish(accs[a], accs_b[a], sp_, i0 + a)
            kb.dma(kb.sp, yT_d[slot], yT[sp_][:], reads=[yT_b[sp_]], writes=[yT_db], sembuf=yT_b[sp_])


def core_consts(c):
    k = np.arange(128)[:, None, None]
    q = np.arange(128)[None, None, :]
    r8 = np.arange(8)
    cp8 = (c - r8) % 8
    cp = cp8[None, :, None]
    fm = np.where((cp < c) | ((cp == c) & (k <= q)), 0.0, NEG).astype(np.float32)
    fmask = np.ascontiguousarray(fm.reshape(128, 8 * 128)).astype(ml_dtypes.bfloat16)
    dm = np.repeat(np.arange(3), 8)[None, :, None]
    cc = np.tile(cp8, 3)[None, :, None]
    dist = (8 * dm + c - cc) * 128 + q - k
    out = {"fmask": fmask, "ndist": np.ascontiguousarray((-dist).astype(np.float32).reshape(128, 24 * 128))}
    for g, (win, r) in enumerate(GROUPS):
        ok = (dist >= 0) & (dist <= win) & (dist % r == 0)
        out[f"vm{g}"] = np.ascontiguousarray(np.where(ok, 0.0, NEG).astype(np.float32).reshape(128, 24 * 128))
    glob = (cp8[:, None] + 8 * np.arange(8)[None, :]).reshape(64)
    j = np.arange(8)[None, :]
    out["LT"] = np.ascontiguousarray((glob[:, None] < 8 * j + c).astype(np.float32))
    out["LTF"] = np.ascontiguousarray((glob[:, None] < glob[None, :]).astype(np.float32))
    i8 = np.zeros((8, 8, 128), np.float32)
    for jj in range(8):
        i8[jj, jj] = 1.0
    out["i8e"] = np.ascontiguousarray(i8.reshape(8, TL)).astype(ml_dtypes.bfloat16)
    return out


CONST_SPECS = (("fmask", [128, 1024], BF16), ("ndist", [128, 3072], F32), ("vm0", [128, 3072], F32),
               ("vm1", [128, 3072], F32), ("vm2", [128, 3072], F32), ("LT", [64, 8], F32), ("LTF", [64, 64], F32), ("i8e", [8, TL], BF16))


def emit_mid(kb, x, g1, w_g, w_up_a, w_up_b, w_out, g2, yT_d, yT_db, merged_d, h_d, h_db, hnT_d, hnT_db):
    nc = kb.nc
    merged_db = Buf("merged_d")
    wua_v = w_up_a.rearrange("(s p) c -> p s c", p=128)
    wub_v = w_up_b.rearrange("(s p) c -> p s c", p=128)
    wo_v = w_out.rearrange("(fc p) c -> p fc c", p=128)
    kb.begin_phase()
    ident, ident_b = make_identity(kb, F32)
    gcol = kb.sb([128, NDC], F32); gcol_b = Buf("gcol")
    kb.dma(kb.sp, gcol[:], g1[:, :], writes=[gcol_b])
    xnT = kb.sb([128, NDC, TL], BF16, "xnT")
    xnT_bufs = [Buf(f"xnT{t}") for t in range(NTT)]
    yT = kb.sb([128, 20, TL], BF16, "yTall"); yT_b = Buf("yTall")
    yT_src = yT_d.rearrange("s p t -> p s t")
    for lo, hi in ((0, 10), (10, 20)):
        kb.dma(kb.sp, yT[:, lo:hi, :], yT_src[:, lo:hi, :], reads=[yT_db], writes=[yT_b])
    kb.begin_phase()
    xs_tiles = [kb.sb([128, D], F32) for _ in range(2)]
    xs_bufs = [Buf(f"xs{i}") for i in range(2)]
    junk = kb.sb([128, D], BF16); junk_b = Buf("junk")
    stat = [kb.sb([128, 4], F32) for _ in range(2)]
    stat_b = [Buf(f"stat{i}") for i in range(2)]
    psT = [kb.ps([128, 512], F32) for _ in range(2)]
    psT_b = [Buf(f"psT{i}") for i in range(2)]
    emit_norm_transpose(kb, x, gcol, gcol_b, xnT, xnT_bufs, ident, ident_b, psT, psT_b, NTT,
                        xs_tiles, xs_bufs, junk, junk_b, stat, stat_b)
    kb.end_phase()
    CW = 256
    wga = [kb.sb([128, NDC, CW], BF16, f"wga{i}") for i in range(2)]
    wgb = [kb.sb([128, NDC, CW], BF16, f"wgb{i}") for i in range(2)]
    wua = [kb.sb([128, 6, CW], BF16, f"wua{i}") for i in range(2)]
    wub = [kb.sb([128, 14, CW], BF16, f"wub{i}") for i in range(2)]
    wg_b = [Buf(f"wgrp{i}") for i in range(2)]
    pg = [[kb.ps([128, 512], F32) for _ in range(4)] for _ in range(2)]
    pg_b = [[Buf(f"pg{i}{k}") for k in range(4)] for i in range(2)]
    sg = [[kb.sb([128, 512], F32) for _ in range(2)] for _ in range(2)]
    sg_b = [[Buf(f"sg{i}{k}") for k in range(2)] for i in range(2)]
    mst = [kb.sb([128, 512], BF16) for _ in range(4)]
    mst_b = [Buf(f"mst{i}") for i in range(4)]

    def load_grp(gi):
        p = gi % 2
        c0 = gi * CW
        half = NDC * CW
        for part in range(2):
            sl = slice(part * 16, (part + 1) * 16)
            kb.dma(kb.pool, wga[p][:, sl, :], w_g[gi][:, part * 16 * CW:(part + 1) * 16 * CW].rearrange("p (a b) -> p a b", b=CW), writes=[wg_b[p]])
            kb.dma(kb.pool, wgb[p][:, sl, :], w_g[gi][:, half + part * 16 * CW:half + (part + 1) * 16 * CW].rearrange("p (a b) -> p a b", b=CW), writes=[wg_b[p]])
        kb.dma(kb.pool, wua[p][:], wua_v[:, :, c0:c0 + CW], writes=[wg_b[p]])
        kb.dma(kb.pool, wub[p][:], wub_v[:, :, c0:c0 + CW], writes=[wg_b[p]])

    NG = D // CW
    load_grp(0)
    it = 0
    for gi in range(NG):
        if gi + 1 < NG:
            load_grp(gi + 1)
        p = gi % 2
        for fcl in range(CW // 128):
            fc = gi * (CW // 128) + fcl
            cs = slice(fcl * 128, (fcl + 1) * 128)
            for th in range(TL // 512):
                q = it % 2; it += 1
                ts = slice(th * 512, (th + 1) * 512)
                xb = xnT_bufs[th * 4:(th + 1) * 4]
                for dc in range(NDC):
                    kb.op(kb.pe, lambda e: e.matmul(pg[q][0][:], lhsT=wga[p][:, dc, cs], rhs=xnT[:, dc, ts],
                                                    start=(dc == 0), stop=(dc == NDC - 1)), reads=[wg_b[p]] + xb, writes=[pg_b[q][0]])
                for dc in range(NDC):
                    kb.op(kb.pe, lambda e: e.matmul(pg[q][1][:], lhsT=wgb[p][:, dc, cs], rhs=xnT[:, dc, ts],
                                                    start=(dc == 0), stop=(dc == NDC - 1)), reads=[wg_b[p]] + xb, writes=[pg_b[q][1]])
                for sl in range(6):
                    kb.op(kb.pe, lambda e: e.matmul(pg[q][2][:], lhsT=wua[p][:, sl, cs], rhs=yT[:, sl, ts],
                                                    start=(sl == 0), stop=(sl == 5)), reads=[wg_b[p], yT_b], writes=[pg_b[q][2]])
                for sl in range(14):
                    kb.op(kb.pe, lambda e: e.matmul(pg[q][3][:], lhsT=wub[p][:, sl, cs], rhs=yT[:, 6 + sl, ts],
                                                    start=(sl == 0), stop=(sl == 13)), reads=[wg_b[p], yT_b], writes=[pg_b[q][3]])
                for k in range(2):
                    kb.op(kb.act, lambda e: e.activation(out=sg[q][k][:], in_=pg[q][k][:], func=AF.Sigmoid),
                          reads=[pg_b[q][k]], writes=[sg_b[q][k]])
                    kb.op(kb.dve, lambda e: e.tensor_tensor(out=sg[q][k][:], in0=sg[q][k][:], in1=pg[q][2 + k][:], op=ALU.mult),
                          reads=[sg_b[q][k], pg_b[q][2 + k]], writes=[sg_b[q][k]])
                m_ = it % 4
                kb.op(kb.dve, lambda e: e.tensor_tensor(out=mst[m_][:], in0=sg[q][0][:], in1=sg[q][1][:], op=ALU.add),
                      reads=[sg_b[q][0], sg_b[q][1]], writes=[mst_b[m_]])
                kb.dma(kb.sp, merged_d[fc, :, ts], mst[m_][:], reads=[mst_b[m_]], writes=[merged_db], sembuf=mst_b[m_])
    kb.end_phase()

    kb.begin_phase()
    mT = kb.sb([128, NDC, TL], BF16, "mT"); mT_b = Buf("mT")
    mT_src = merged_d.rearrange("f p t -> p f t")
    for lo in (0, NDC // 2):
        kb.dma(kb.sp, mT[:, lo:lo + NDC // 2, :], mT_src[:, lo:lo + NDC // 2, :], reads=[merged_db], writes=[mT_b])
    wo = [kb.sb([128, NDC, 512], BF16, f"wo{i}") for i in range(2)]
    wo_b = [Buf(f"wo{i}") for i in range(2)]
    po = [kb.ps([128, 512], F32) for _ in range(4)]
    po_b = [Buf(f"po{i}") for i in range(4)]
    xr = [kb.sb([128, 512], F32) for _ in range(4)]
    xr_b = [Buf(f"xr{i}") for i in range(4)]

    def load_wo(cb):
        for part in range(4):
            kb.dma(kb.pool, wo[cb % 2][:, part * 8:(part + 1) * 8, :], wo_v[:, part * 8:(part + 1) * 8, cb * 512:(cb + 1) * 512],
                   writes=[wo_b[cb % 2]])
    load_wo(0)
    it = 0
    for cb in range(D // 512):
        if cb + 1 < D // 512:
            load_wo(cb + 1)
        for tt in range(NTT):
            q = it % 4; it += 1
            kb.dma(kb.sp, xr[q][:], x[tt * 128:(tt + 1) * 128, cb * 512:(cb + 1) * 512], writes=[xr_b[q]])
            for fc in range(NDC):
                kb.op(kb.pe, lambda e: e.matmul(po[q][:], lhsT=mT[:, fc, tt * 128:(tt + 1) * 128], rhs=wo[cb % 2][:, fc, :],
                                                start=(fc == 0), stop=(fc == NDC - 1)), reads=[mT_b, wo_b[cb % 2]], writes=[po_b[q]])
            kb.op(kb.dve, lambda e: e.tensor_tensor(out=xr[q][:], in0=po[q][:], in1=xr[q][:], op=ALU.add),
                  reads=[po_b[q], xr_b[q]], writes=[xr_b[q]])
            kb.dma(kb.sp, h_d[tt * 128:(tt + 1) * 128, cb * 512:(cb + 1) * 512], xr[q][:], reads=[xr_b[q]], writes=[h_db], sembuf=xr_b[q])
    kb.end_phase()

    kb.begin_phase()
    ident, ident_b = make_identity(kb, F32)
    gcol = kb.sb([128, NDC], F32); gcol_b = Buf("gcol2")
    kb.dma(kb.sp, gcol[:], g2[:, :], writes=[gcol_b])
    hnT = kb.sb([128, NDC, TL], BF16, "hnT")
    hnT_bufs = [Buf(f"hnT{t}") for t in range(NTT)]
    xs_tiles = [kb.sb([128, D], F32) for _ in range(2)]
    xs_bufs = [Buf(f"hs{i}") for i in range(2)]
    junk = kb.sb([128, D], BF16); junk_b = Buf("junk")
    stat = [kb.sb([128, 4], F32) for _ in range(2)]
    stat_b = [Buf(f"stat{i}") for i in range(2)]
    psT = [kb.ps([128, 512], F32) for _ in range(2)]
    psT_b = [Buf(f"psT{i}") for i in range(2)]
    kb.drain(kb.sp, [h_db])
    emit_norm_transpose(kb, h_d, gcol, gcol_b, hnT, hnT_bufs, ident, ident_b, psT, psT_b, NTT,
                        xs_tiles, xs_bufs, junk, junk_b, stat, stat_b)
    for dc in range(NDC):
        kb.dma(kb.sp, hnT_d[dc], hnT[:, dc, :], reads=hnT_bufs, writes=[hnT_db], sembuf=hnT_bufs[0])
    kb.end_phase()


def emit_peer(kb, hnT_d, hnT_db, h_d, h_db, w_pq, skT, uT, v, iota_d, sc_d, G_d, act_d, y, y_db, NJ=128):
    nc = kb.nc
    wq_v = w_pq.rearrange("(dc p) c -> p dc c", p=128)
    sc_db, G_db = Buf("sc_d"), Buf("G_d")
    kb.begin_phase()
    identf, identf_b = make_identity(kb, F32)
    stT = kb.sb([128, 4, TL], F32, "stT"); stT_b = Buf("stT")

    kb.begin_phase()
    qpT = kb.sb([128, 16, TL], BF16, "qpT"); qpT_b = Buf("qpT")
    skb = kb.sb([128, 16, 128], BF16, "skb"); skb_b = Buf("skb")
    kb.dma(kb.pool, skb[:], skT.rearrange("k q n -> q k n"), writes=[skb_b])
    kb.begin_phase()
    hnT = kb.sb([128, NDC, TL], BF16, "hnT"); hnT_b = Buf("hnT")
    hnT_src = hnT_d.rearrange("d p t -> p d t")
    for lo in (0, NDC // 2):
        kb.dma(kb.sp, hnT[:, lo:lo + NDC // 2, :], hnT_src[:, lo:lo + NDC // 2, :], reads=[hnT_db], writes=[hnT_b])
    wq = [kb.sb([128, NDC, 512], BF16, f"wq{i}") for i in range(2)]
    wq_b = [Buf(f"wq{i}") for i in range(2)]
    pq = [kb.ps([128, 512], F32) for _ in range(2)]
    pq_b = [Buf(f"pq{i}") for i in range(2)]

    def load_wq(g):
        for part in range(4):
            kb.dma(kb.pool, wq[g % 2][:, part * 8:(part + 1) * 8, :], wq_v[:, part * 8:(part + 1) * 8, g * 512:(g + 1) * 512],
                   writes=[wq_b[g % 2]])
    load_wq(0)
    it = 0
    for g in range(4):
        if g + 1 < 4:
            load_wq(g + 1)
        for kk in range(4):
            k = g * 4 + kk
            for th in range(TL // 512):
                q = it % 2; it += 1
                for dc in range(NDC):
                    kb.op(kb.pe, lambda e: e.matmul(pq[q][:], lhsT=wq[g % 2][:, dc, kk * 128:(kk + 1) * 128],
                                                    rhs=hnT[:, dc, th * 512:(th + 1) * 512], start=(dc == 0), stop=(dc == NDC - 1)),
                          reads=[wq_b[g % 2], hnT_b], writes=[pq_b[q]])
                kb.op(kb.act, lambda e: e.activation(out=qpT[:, k, th * 512:(th + 1) * 512], in_=pq[q][:], func=AF.Copy),
                      reads=[pq_b[q]], writes=[qpT_b])
    kb.end_phase()

    ps_sc = [kb.ps([128, 512], F32) for _ in range(4)]
    ps_sc_b = [Buf(f"pssc{i}") for i in range(4)]
    ps_t = kb.ps([128, 512], F32); ps_t_b = Buf("pst")
    sc = [kb.sb([128, 16 * 128], F32, f"sc{i}") for i in range(2)]
    sc_b = [Buf(f"sc{i}") for i in range(2)]
    t16 = kb.sb([128, 16 * 16], F32); t16_b = Buf("t16")
    tmp = [kb.sb([128, 128], F32) for _ in range(2)]
    tmp_b = [Buf(f"tmp{i}") for i in range(2)]
    idx = kb.sb([128, 8 * 16], U32); idx_b = Buf("idx")
    cand = kb.sb([128, 8 * 256], F32); cand_b = Buf("cand")
    c16 = kb.sb([128, 8 * 16], F32); c16_b = Buf("c16")
    tmp2 = [kb.sb([128, 256], F32) for _ in range(2)]
    tmp2_b = [Buf(f"tmp2{i}") for i in range(2)]
    d16 = kb.sb([128, 128], F32); d16_b = Buf("d16")
    zs = kb.sb([128, 32], F32); zs_b = Buf("zs")
    pk = kb.sb([128, 4 * 128], F32); pk_b = Buf("pk")
    for tt in range(NTT):
        s_, sb_ = sc[tt % 2], sc_b[tt % 2]
        for k in range(16):
            kb.op(kb.pe, lambda e: e.matmul(ps_sc[k // 4][:, (k % 4) * 128:(k % 4 + 1) * 128], lhsT=qpT[:, k, tt * 128:(tt + 1) * 128],
                                            rhs=skb[:, k, :], start=True, stop=True), reads=[qpT_b, skb_b], writes=[ps_sc_b[k // 4]])
        for q in range(4):
            kb.op(kb.act, lambda e: e.activation(out=s_[:, q * 512:(q + 1) * 512], in_=ps_sc[q][:], func=AF.Copy),
                  reads=[ps_sc_b[q]], writes=[sb_])
        sc3 = s_[:].rearrange("p (k n) -> p k n", n=128)
        kb.dma(kb.sp, sc_d[tt * 128:(tt + 1) * 128],
               s_[:].rearrange("p (h c n) -> p h c n", c=2, n=128)[:, :, 1, :], reads=[sb_], writes=[sc_db], sembuf=sb_)
        t163 = t16[:].rearrange("p (k a) -> p k a", a=16)
        idx3 = idx[:].rearrange("p (h a) -> p h a", a=16)
        for k in range(16):
            tm, tmb = tmp[k % 2], tmp_b[k % 2]
            kb.op(kb.dve, lambda e: e.max(out=t163[:, k, 0:8], in_=sc3[:, k, :]), reads=[sb_], writes=[t16_b])
            kb.op(kb.dve, lambda e: e.match_replace(out=tm[:], in_to_replace=t163[:, k, 0:8], in_values=sc3[:, k, :], imm_value=-1e30),
                  reads=[sb_, t16_b], writes=[tmb])
            kb.op(kb.dve, lambda e: e.max(out=t163[:, k, 8:16], in_=tm[:]), reads=[tmb], writes=[t16_b])
            if k % 2 == 0:
                kb.op(kb.dve, lambda e: e.max_index(out=idx3[:, k // 2, 0:8], in_max=t163[:, k, 0:8], in_values=sc3[:, k, :]),
                      reads=[sb_, t16_b], writes=[idx_b])
                kb.op(kb.dve, lambda e: e.max_index(out=idx3[:, k // 2, 8:16], in_max=t163[:, k, 8:16], in_values=tm[:]),
                      reads=[tmb, t16_b], writes=[idx_b])
        t16v = t16[:].rearrange("p (h c a) -> p h c a", c=2, a=16)
        kb.op(kb.dve, lambda e: e.tensor_tensor(out=cand[:].rearrange("p (h a b) -> p h a b", a=16, b=16),
                                                in0=t16v[:, :, 0, :].unsqueeze(3).to_broadcast([128, 8, 16, 16]),
                                                in1=t16v[:, :, 1, :].unsqueeze(2).to_broadcast([128, 8, 16, 16]), op=ALU.add),
              reads=[t16_b], writes=[cand_b])
        cand3 = cand[:].rearrange("p (h n) -> p h n", n=256)
        c163 = c16[:].rearrange("p (h a) -> p h a", a=16)
        for h in range(8):
            tm, tmb = tmp2[h % 2], tmp2_b[h % 2]
            kb.op(kb.dve, lambda e: e.max(out=c163[:, h, 0:8], in_=cand3[:, h, :]), reads=[cand_b], writes=[c16_b])
            kb.op(kb.dve, lambda e: e.match_replace(out=tm[:], in_to_replace=c163[:, h, 0:8], in_values=cand3[:, h, :], imm_value=-1e30),
                  reads=[cand_b, c16_b], writes=[tmb])
            kb.op(kb.dve, lambda e: e.max(out=c163[:, h, 8:16], in_=tm[:]), reads=[tmb], writes=[c16_b])
        kb.op(kb.dve, lambda e: e.tensor_tensor(out=d16[:].rearrange("p (h a) -> p h a", a=16), in0=c163,
                                                in1=c163[:, :, 0:1].to_broadcast([128, 8, 16]), op=ALU.subtract),
              reads=[c16_b], writes=[d16_b])
        kb.op(kb.act, lambda e: e.activation(out=d16[:], in_=d16[:], func=AF.Exp), reads=[d16_b], writes=[d16_b])
        kb.op(kb.dve, lambda e: e.reduce_sum(out=zs[:, 0:8], in_=d16[:].rearrange("p (h a) -> p h a", a=16), axis=AX.X),
              reads=[d16_b], writes=[zs_b])
        kb.op(kb.act, lambda e: e.activation(out=zs[:, 8:16], in_=zs[:, 0:8], func=AF.Ln), reads=[zs_b], writes=[zs_b])
        kb.op(kb.dve, lambda e: e.tensor_tensor(out=zs[:, 16:24], in0=c163[:, :, 0], in1=zs[:, 8:16], op=ALU.add),
              reads=[c16_b, zs_b], writes=[zs_b])
        pk4 = pk[:].rearrange("p (q h a) -> p q h a", q=4, a=16)
        kb.op(kb.dve, lambda e: e.tensor_copy(out=pk4[:, 0], in_=t16v[:, :, 0, :]), reads=[t16_b], writes=[pk_b])
        kb.op(kb.dve, lambda e: e.tensor_copy(out=pk4[:, 1], in_=c163[:, :, 15:16].to_broadcast([128, 8, 16])),
              reads=[c16_b], writes=[pk_b])
        kb.op(kb.dve, lambda e: e.tensor_copy(out=pk4[:, 2], in_=zs[:, 16:24].unsqueeze(2).to_broadcast([128, 8, 16])),
              reads=[zs_b], writes=[pk_b])
        kb.op(kb.dve, lambda e: e.tensor_copy(out=pk4[:, 3], in_=idx3), reads=[idx_b], writes=[pk_b])
        for q in range(4):
            kb.op(kb.pe, lambda e: e.transpose(ps_t[:, q * 128:(q + 1) * 128], pk[:, q * 128:(q + 1) * 128], identf[:]),
                  reads=[pk_b, identf_b], writes=[ps_t_b])
        kb.op(kb.act, lambda e: e.activation(out=stT[:, :, tt * 128:(tt + 1) * 128], in_=ps_t[:].rearrange("p (q t) -> p q t", q=4),
                                             func=AF.Copy), reads=[ps_t_b], writes=[stT_b])
    kb.end_phase()

    kb.begin_phase()
    iota = kb.sb([128, 128], F32); iota_b = Buf("iota")
    kb.dma(kb.sp, iota[:], iota_d[:, :], writes=[iota_b])
    TC = 32
    scr = [kb.sb([128, TC, 128], F32, f"scr{i}") for i in range(2)]
    scr_b = [Buf(f"scr{i}") for i in range(2)]
    mk = [kb.sb([128, TC, 128], BF16, f"mk{i}") for i in range(2)]
    mk_b = [Buf(f"mk{i}") for i in range(2)]
    cb_ = [kb.sb([128, TC, 128], BF16, f"cb{i}") for i in range(2)]
    cb_b = [Buf(f"cb{i}") for i in range(2)]
    oh = [kb.sb([128, TC, 128], BF16, f"oh{i}") for i in range(2)]
    oh_b = [Buf(f"oh{i}") for i in range(2)]
    Gs = [kb.sb([128, 128, 128], BF16, f"Gs{i}") for i in range(2)]
    Gs_b = [Buf(f"Gs{i}") for i in range(2)]
    pG = [kb.ps([128, 512], F32) for _ in range(4)]
    pG_b = [Buf(f"pG{i}") for i in range(4)]
    kb.drain(kb.sp, [sc_db])
    nev = 0
    kb.drain(kb.pool, [sc_db])

    def load_scr(ch):
        t0 = ch * TC
        p = ch % 2
        for h in range(8):
            kb.dma(kb.sp if h % 2 == 0 else kb.pool, scr[p][h * 16:(h + 1) * 16],
                   sc_d[t0:t0 + TC, h, :].unsqueeze(0).to_broadcast([16, TC, 128]), reads=[sc_db], writes=[scr_b[p]])
    load_scr(0)
    for ch in range(TL // TC):
        t0 = ch * TC
        p = ch % 2
        if ch + 1 < TL // TC:
            load_scr(ch + 1)
        bc = lambda q: stT[:, q, t0:t0 + TC].unsqueeze(2).to_broadcast([128, TC, 128])
        kb.op(kb.dve, lambda e: e.tensor_tensor(out=oh[p][:], in0=iota[:].unsqueeze(1).to_broadcast([128, TC, 128]), in1=bc(3), op=ALU.is_equal),
              reads=[iota_b, stT_b], writes=[oh_b[p]])
        kb.op(kb.dve, lambda e: e.tensor_tensor(out=scr[p][:], in0=scr[p][:], in1=bc(0), op=ALU.add),
              reads=[scr_b[p], stT_b], writes=[scr_b[p]])
        kb.op(kb.dve, lambda e: e.tensor_tensor(out=mk[p][:], in0=scr[p][:], in1=bc(1), op=ALU.is_ge),
              reads=[scr_b[p], stT_b], writes=[mk_b[p]])
        kb.op(kb.dve, lambda e: e.tensor_tensor(out=scr[p][:], in0=scr[p][:], in1=bc(2), op=ALU.subtract),
              reads=[scr_b[p], stT_b], writes=[scr_b[p]])
        kb.op(kb.act, lambda e: e.activation(out=scr[p][:], in_=scr[p][:], func=AF.Exp), reads=[scr_b[p]], writes=[scr_b[p]])
        kb.op(kb.dve, lambda e: e.tensor_tensor(out=cb_[p][:], in0=scr[p][:], in1=mk[p][:], op=ALU.mult),
              reads=[scr_b[p], mk_b[p]], writes=[cb_b[p]])
        gsi = (t0 // 128) % 2
        for t4 in range(TC // 4):
            bk = nev % 4; nev += 1
            for i in range(4):
                t = t4 * 4 + i
                kb.op(kb.pe, lambda e: e.matmul(pG[bk][:, i * 128:(i + 1) * 128], lhsT=oh[p][:, t, :], rhs=cb_[p][:, t, :], start=True, stop=True),
                      reads=[oh_b[p], cb_b[p]], writes=[pG_b[bk]])
            tl = (t0 % 128) + t4 * 4
            kb.op(kb.act, lambda e: e.activation(out=Gs[gsi][:, :, tl:tl + 4].rearrange("p j t -> p t j"),
                                                 in_=pG[bk][:].rearrange("p (t j) -> p t j", t=4), func=AF.Copy),
                  reads=[pG_b[bk]], writes=[Gs_b[gsi]])
        if (t0 + TC) % 128 == 0:
            tb = t0 // 128
            kb.dma(kb.sp, G_d[:, :, tb * 128:(tb + 1) * 128].rearrange("j p t -> p j t"), Gs[gsi][:],
                   reads=[Gs_b[gsi]], writes=[G_db], sembuf=Gs_b[gsi])
    kb.end_phase()

    kb.end_phase()
    act_db = Buf("act_d")
    kb.begin_phase()
    UG = 4
    hnT = kb.sb([128, NDC, TL], BF16, "hnT"); hnT_b = Buf("hnT")
    hnT_src = hnT_d.rearrange("d p t -> p d t")
    for lo in (0, NDC // 2):
        kb.dma(kb.sp, hnT[:, lo:lo + NDC // 2, :], hnT_src[:, lo:lo + NDC // 2, :], reads=[hnT_db], writes=[hnT_b])
    ug = [kb.sb([128, NDC, UG * 128], BF16, f"ug{i}") for i in range(2)]
    gg = [kb.sb([128, UG, TL], BF16, f"gg{i}") for i in range(2)]
    grp_b = [Buf(f"grp{i}") for i in range(2)]
    ge = [kb.sb([128, 512], F32) for _ in range(2)]
    ge_b = [Buf(f"ge{i}") for i in range(2)]
    aT = [kb.sb([128, UG, TL], BF16, f"aT{i}") for i in range(2)]
    aT_b = [Buf(f"aT{i}") for i in range(2)]
    pa = [kb.ps([128, 512], F32) for _ in range(4)]
    pa_b = [Buf(f"pa{i}") for i in range(4)]
    kb.drain(kb.sp, [G_db])
    NG1 = NJ // UG

    def load_grp(g):
        p = g % 2
        for part in range(4):
            kb.dma(kb.pool, ug[p][:, part * 8:(part + 1) * 8, :],
                   uT[g][:, part * 8 * 512:(part + 1) * 8 * 512].rearrange("p (a b) -> p a b", b=512), writes=[grp_b[p]])
        kb.dma(kb.sp, gg[p][:], G_d[g * UG:(g + 1) * UG].rearrange("j p t -> p j t"), reads=[G_db], writes=[grp_b[p]])
    load_grp(0)
    it = 0
    for g in range(NG1):
        if g + 1 < NG1:
            load_grp(g + 1)
        p = g % 2
        for jj in range(UG):
            for th in range(TL // 512):
                q = it % 4; it += 1
                ts = slice(th * 512, (th + 1) * 512)
                for dc in range(NDC):
                    kb.op(kb.pe, lambda e: e.matmul(pa[q][:], lhsT=ug[p][:, dc, jj * 128:(jj + 1) * 128], rhs=hnT[:, dc, ts],
                                                    start=(dc == 0), stop=(dc == NDC - 1)), reads=[grp_b[p], hnT_b], writes=[pa_b[q]])
                kb.op(kb.act, lambda e: e.activation(out=ge[q % 2][:], in_=pa[q][:], func=AF.Gelu), reads=[pa_b[q]], writes=[ge_b[q % 2]])
                kb.op(kb.dve, lambda e: e.tensor_tensor(out=aT[p][:, jj, ts], in0=ge[q % 2][:], in1=gg[p][:, jj, ts], op=ALU.mult),
                      reads=[ge_b[q % 2], grp_b[p]], writes=[aT_b[p]])
        kb.dma(kb.sp, act_d[g * UG:(g + 1) * UG].rearrange("j p t -> p j t"), aT[p][:], reads=[aT_b[p]], writes=[act_db], sembuf=aT_b[p])
    kb.end_phase()

    kb.begin_phase()
    VG = 8
    NBUF = 3
    vg = [kb.sb([128, VG, 512], BF16, f"vg{i}") for i in range(NBUF)]
    ag = [kb.sb([128, VG, TL], BF16, f"ag{i}") for i in range(NBUF)]
    vgrp_b = [Buf(f"vgrp{i}") for i in range(NBUF)]
    agrp_b = [Buf(f"agrp{i}") for i in range(NBUF)]
    po = [kb.ps([128, 512], F32) for _ in range(8)]
    po_b = [Buf(f"po{i}") for i in range(8)]
    hx = [kb.sb([128, 512], F32) for _ in range(8)]
    hx_b = [Buf(f"hx{i}") for i in range(8)]
    kb.drain(kb.sp, [act_db])
    kb.drain(kb.act, [act_db])
    NG2 = NJ // VG
    work = [(cb, jg) for cb in range(D // 512) for jg in range(NG2)]

    def load_v(wi):
        cb, jg = work[wi]
        p = wi % NBUF
        for hf in range(2):
            kb.dma(kb.pool, vg[p][:, hf * 4:(hf + 1) * 4, :],
                   v[cb, jg][:, hf * 2048:(hf + 1) * 2048].rearrange("p (a b) -> p a b", b=512), writes=[vgrp_b[p]])
        kb.dma(kb.act, ag[p][:, 0:4, :], act_d[jg * VG:jg * VG + 4].rearrange("j p t -> p j t"), reads=[act_db], writes=[agrp_b[p]])
        kb.dma(kb.sp, ag[p][:, 4:8, :], act_d[jg * VG + 4:jg * VG + 8].rearrange("j p t -> p j t"), reads=[act_db], writes=[agrp_b[p]])
    for wi in range(min(NBUF - 1, len(work))):
        load_v(wi)
    nh = 0
    for wi, (cb, jg) in enumerate(work):
        if wi + NBUF - 1 < len(work):
            load_v(wi + NBUF - 1)
        p = wi % NBUF
        if jg == NG2 - 1:
            for tt in range(NTT):
                kb.dma(kb.sp, hx[tt][:], h_d[tt * 128:(tt + 1) * 128, cb * 512:(cb + 1) * 512], reads=[h_db], writes=[hx_b[tt]])
        for tt in range(NTT):
            for jj in range(VG):
                j = jg * VG + jj
                kb.op(kb.pe, lambda e: e.matmul(po[tt][:], lhsT=ag[p][:, jj, tt * 128:(tt + 1) * 128], rhs=vg[p][:, jj, :],
                                                start=(j == 0), stop=(j == NJ - 1)), reads=[vgrp_b[p], agrp_b[p]], writes=[po_b[tt]])
        if jg == NG2 - 1:
            for tt in range(NTT):
                q = tt
                kb.op(kb.dve, lambda e: e.tensor_tensor(out=hx[q][:], in0=po[tt][:], in1=hx[q][:], op=ALU.add),
                      reads=[po_b[tt], hx_b[q]], writes=[hx_b[q]])
                kb.dma(kb.sp, y[tt * 128:(tt + 1) * 128, cb * 512:(cb + 1) * 512], hx[q][:], reads=[hx_b[q]], writes=[y_db], sembuf=hx_b[q])
    kb.end_phase()


NEED_R = (2, 5, 8)


def front_blocks():
    blocks = []

    def add_heads(c0, nh, kind, gc, hb, need):
        h = 0
        while h < nh:
            n = min(4, nh - h)
            blocks.append((c0 + h * 128, n * 128, kind, gc, hb + h, need))
            h += n
    for g in range(3):
        add_heads(C_KA + g * 6 * 128, 6, "k", 1, g * 6, NEED_R[g])
    add_heads(C_KB, NHB, "k", 3, NHA, 8)
    for g in range(3):
        add_heads(C_VA + g * 6 * 128, 6, "v", None, g * 6, NEED_R[g])
    add_heads(C_VB, NHB, "v", None, NHA, 8)
    blocks.append((C_F, NHB, "f", None, 0, 8))
    nkv = len(blocks)
    add_heads(C_QA, NHA, "q", 0, 0, 0)
    add_heads(C_QB, NHB, "q", 2, NHA, 0)
    return blocks, nkv


def host_wblk(w_in):
    blocks, _ = front_blocks()
    out = np.zeros((len(blocks), 128, NDC * 512), np.float32)
    o4 = out.reshape(len(blocks), 128, NDC, 512)
    for bi, (c0, ncol, *_r) in enumerate(blocks):
        o4[bi, :, :, :ncol] = w_in[:, c0:c0 + ncol].reshape(NDC, 128, ncol).transpose(1, 0, 2)
    return out


def emit_front(kb, xall, x_own, wblk, g1, hg, bfg, qT_d, kall_d, vall_d, lfall_d, out_b):
    blocks, nkv = front_blocks()
    kb.begin_phase()
    ident, ident_b = make_identity(kb, F32)
    ones = kb.sb([128, 128], F32); ones_b = Buf("ones")
    kb.op(kb.pool, lambda e: e.memset(ones[:], 1.0), writes=[ones_b])
    gcol = kb.sb([128, NDC], F32); gcol_b = Buf("gcol")
    kb.dma(kb.sp, gcol[:], g1[:, :], writes=[gcol_b])
    hgc = kb.sb([128, 4], F32); hgc_b = Buf("hgc")
    kb.dma(kb.sp, hgc[:], hg[:, :], writes=[hgc_b])
    for col in (0, 2):
        kb.op(kb.dve, lambda e: e.tensor_scalar_mul(out=hgc[:, col:col + 1], in0=hgc[:, col:col + 1], scalar1=HD ** -0.5),
              reads=[hgc_b], writes=[hgc_b])
    bfs = kb.sb([128, NHB], F32); bfs_b = Buf("bfs")
    kb.dma(kb.sp, bfs[:], bfg[:, :], writes=[bfs_b])
    xnT = kb.sb([128, NDC, TL], BF16, "xnT")
    xnT_bufs = [Buf(f"xnT{t}") for t in range(NTT)]
    xs_tiles = [kb.sb([128, D], F32) for _ in range(2)]
    xs_bufs = [Buf(f"xs{i}") for i in range(2)]
    junk = kb.sb([128, D], BF16); junk_b = Buf("junk")
    stat = [kb.sb([128, 4], F32) for _ in range(2)]
    stat_b = [Buf(f"stat{i}") for i in range(2)]
    psT = [kb.ps([128, 512], F32) for _ in range(2)]
    psT_b = [Buf(f"psT{i}") for i in range(2)]
    wt = [kb.sb([128, NDC, 512], BF16, f"wt{i}") for i in range(2)]
    wt_b = [Buf(f"wt{i}") for i in range(2)]
    pacc = [kb.ps([128, 512], F32) for _ in range(2)]
    pacc_b = [Buf(f"pacc{i}") for i in range(2)]
    pss = [kb.ps([128, 512], F32) for _ in range(2)]
    pss_b = [Buf(f"pss{i}") for i in range(2)]
    sq = [kb.sb([128, 512], F32) for _ in range(2)]
    sq_b = [Buf(f"sq{i}") for i in range(2)]
    rs = [kb.sb([128, 512], F32) for _ in range(2)]
    rs_b = [Buf(f"rs{i}") for i in range(2)]
    stg = [kb.sb([128, 512], BF16) for _ in range(4)]
    stg_b = [Buf(f"stg{i}") for i in range(4)]

    work = []
    for cc in range(NCORES):
        work += [(cc, bi) for bi in range(nkv) if cc < blocks[bi][5]]
        if cc == 0:
            work += [(0, bi) for bi in range(nkv, len(blocks))]

    def load_w(wi):
        bi = work[wi][1]
        for part in range(4):
            kb.dma(kb.pool, wt[wi % 2][:, part * 8:(part + 1) * 8, :],
                   wblk[bi][:, part * 8 * 512:(part + 1) * 8 * 512].rearrange("p (a b) -> p a b", b=512), writes=[wt_b[wi % 2]])

    cnt = {"acc": 0, "ss": 0, "stg": 0}
    load_w(0)
    cur_pass = -1
    staged = ()
    for wi, (ps_, bi) in enumerate(work):
        if ps_ != cur_pass:
            cur_pass = ps_
            src = xall[ps_ * TL:(ps_ + 1) * TL, :]
            emit_norm_transpose(kb, src, gcol, gcol_b, xnT, xnT_bufs, ident, ident_b, psT, psT_b, NTT,
                                xs_tiles, xs_bufs, junk, junk_b, stat, stat_b, staged=staged)
            staged = ()
        if wi + 1 < len(work):
            load_w(wi + 1)
            if work[wi + 1][0] != ps_:
                nsrc = xall[(ps_ + 1) * TL:(ps_ + 2) * TL, :]
                for tt0 in (0, 1):
                    _nt_stage1(kb, nsrc, tt0, xs_tiles, xs_bufs, junk, junk_b, stat, stat_b)
                staged = (0, 1)
        c0, ncol, kind, gc, hb, _need = blocks[bi]
        tok0 = ps_ * TL
        w, wb = wt[wi % 2], wt_b[wi % 2]
        if kind in ("q", "k"):
            for hh in range(ncol // 128):
                for th in range(TL // 512):
                    a = cnt["acc"] % 2; cnt["acc"] += 1
                    pa, pab = pacc[a], pacc_b[a]
                    for dc in range(NDC):
                        kb.op(kb.pe, lambda e: e.matmul(pa[:], lhsT=w[:, dc, hh * 128:(hh + 1) * 128],
                                                        rhs=xnT[:, dc, th * 512:(th + 1) * 512],
                                                        start=(dc == 0), stop=(dc == NDC - 1)),
                              reads=[wb] + xnT_bufs[th * 4:(th + 1) * 4], writes=[pab])
                    s_ = cnt["ss"] % 2; cnt["ss"] += 1
                    kb.op(kb.act, lambda e: e.activation(out=sq[s_][:], in_=pa[:], func=AF.Square), reads=[pab], writes=[sq_b[s_]])
                    kb.op(kb.pe, lambda e: e.matmul(pss[s_][:], lhsT=ones[:], rhs=sq[s_][:], start=True, stop=True),
                          reads=[ones_b, sq_b[s_]], writes=[pss_b[s_]])
                    kb.op(kb.act, lambda e: e.activation(out=rs[s_][:], in_=pss[s_][:], func=AF.Sqrt, scale=1.0 / HD, bias=EPS),
                          reads=[pss_b[s_]], writes=[rs_b[s_]])
                    kb.op(kb.dve, lambda e: e.reciprocal(out=rs[s_][:], in_=rs[s_][:]), reads=[rs_b[s_]], writes=[rs_b[s_]])
                    g_ = cnt["stg"] % 4; cnt["stg"] += 1
                    kb.op(kb.dve, lambda e: e.scalar_tensor_tensor(out=stg[g_][:], in0=pa[:], scalar=hgc[:, gc:gc + 1],
                                                                   in1=rs[s_][:], op0=ALU.mult, op1=ALU.mult),
                          reads=[pab, hgc_b, rs_b[s_]], writes=[stg_b[g_]])
                    if kind == "q":
                        dst = qT_d[hb + hh, :, th * 512:(th + 1) * 512]
                    else:
                        dst = kall_d[hb + hh, :, tok0 + th * 512:tok0 + (th + 1) * 512]
                    kb.dma(kb.sp, dst, stg[g_][:], reads=[stg_b[g_]], writes=[out_b], sembuf=stg_b[g_])
        elif kind == "v":
            nh = ncol // 128
            for tt in range(NTT):
                a = cnt["acc"] % 2; cnt["acc"] += 1
                pa, pab = pacc[a], pacc_b[a]
                for dc in range(NDC):
                    kb.op(kb.pe, lambda e: e.matmul(pa[:, 0:ncol], lhsT=xnT[:, dc, tt * 128:(tt + 1) * 128],
                                                    rhs=w[:, dc, 0:ncol], start=(dc == 0), stop=(dc == NDC - 1)),
                          reads=[wb, xnT_bufs[tt]], writes=[pab])
                g_ = cnt["stg"] % 4; cnt["stg"] += 1
                kb.op(kb.act, lambda e: e.activation(out=stg[g_][:, 0:ncol], in_=pa[:, 0:ncol], func=AF.Copy),
                      reads=[pab], writes=[stg_b[g_]])
                kb.dma(kb.sp, vall_d[hb:hb + nh, :, ps_ * NTT + tt, :].rearrange("h k d -> k h d"),
                       stg[g_][:, 0:ncol].rearrange("t (h d) -> t h d", h=nh),
                       reads=[stg_b[g_]], writes=[out_b], sembuf=stg_b[g_])
        else:
            for tt in range(NTT):
                a = cnt["acc"] % 2; cnt["acc"] += 1
                pa, pab = pacc[a], pacc_b[a]
                for dc in range(NDC):
                    kb.op(kb.pe, lambda e: e.matmul(pa[:, 0:NHB], lhsT=xnT[:, dc, tt * 128:(tt + 1) * 128],
                                                    rhs=w[:, dc, 0:NHB], start=(dc == 0), stop=(dc == NDC - 1)),
                          reads=[wb, xnT_bufs[tt]], writes=[pab])
                s_ = cnt["ss"] % 2; cnt["ss"] += 1
                z = sq[s_]; zb = sq_b[s_]
                kb.op(kb.dve, lambda e: e.tensor_tensor(out=z[:, 0:NHB], in0=pa[:, 0:NHB], in1=bfs[:], op=ALU.add),
                      reads=[pab, bfs_b], writes=[zb])
                kb.op(kb.act, lambda e: e.activation(out=z[:, 0:NHB], in_=z[:, 0:NHB], func=AF.Exp, scale=-1.0), reads=[zb], writes=[zb])
                kb.op(kb.act, lambda e: e.activation(out=z[:, 0:NHB], in_=z[:, 0:NHB], func=AF.Ln, bias=1.0), reads=[zb], writes=[zb])
                kb.op(kb.dve, lambda e: e.tensor_scalar_mul(out=z[:, 16:16 + NHB], in0=z[:, 0:NHB], scalar1=-1.0), reads=[zb], writes=[zb])
                kb.dma(kb.sp, lfall_d[:, ps_ * NTT + tt, :], z[:, 16:16 + NHB],
                       reads=[zb], writes=[out_b], sembuf=zb)
    kb.end_phase()


def build_fused():
    nc = bass.Bass("TRN2", target_bir_lowering=False)
    ext = lambda n, sh, dt: nc.dram_tensor(n, sh, dt, kind="ExternalInput").ap()
    blocks, _ = front_blocks()
    xall = ext("xall", [S, D], F32)
    x = ext("x", [TL, D], F32)
    wblk = ext("wblk", [len(blocks), 128, NDC * 512], F32)
    hg = ext("hg", [128, 4], F32)
    bfg = ext("bfg", [128, NHB], F32)
    cst = {n: ext(n, sh, dt) for n, sh, dt in CONST_SPECS}
    g1 = ext("g1", [128, NDC], F32)
    g2 = ext("g2", [128, NDC], F32)
    w_g = ext("w_g", [16, 128, 2 * NDC * 256], F32)
    w_up_a = ext("w_up_a", [768, D], F32)
    w_up_b = ext("w_up_b", [WB, D], F32)
    w_out = ext("w_out", [D, D], F32)
    w_pq = ext("w_pq", [D, PH * 2 * QH], F32)
    skT = ext("skT", [16, 128, 128], F32)
    uT = ext("uT", [32, 128, NDC * 512], F32)
    v = ext("v", [8, 16, 128, 8 * 512], F32)
    iota_d = ext("iota", [128, 128], F32)
    qT_d = nc.dram_tensor("qT_d", [NH, 128, TL], BF16).ap()
    kall_d = nc.dram_tensor("kall_d", [NH, 128, S], BF16).ap()
    vall_d = nc.dram_tensor("vall_d", [NH, 128, S // 128, 128], BF16).ap()
    lfall_d = nc.dram_tensor("lfall_d", [128, S // 128, NHB], F32).ap()
    yT_d = nc.dram_tensor("yT_d", [20, 128, TL], BF16).ap()
    merged_d = nc.dram_tensor("merged_d", [NDC, 128, TL], BF16).ap()
    h_d = nc.dram_tensor("h_d", [TL, D], F32).ap()
    hnT_d = nc.dram_tensor("hnT_d", [NDC, 128, TL], BF16).ap()
    sc_d = nc.dram_tensor("sc_d", [TL, 8, 128], F32).ap()
    G_d = nc.dram_tensor("G_d", [128, 128, TL], BF16).ap()
    act_d = nc.dram_tensor("act_d", [128, 128, TL], BF16).ap()
    y = nc.dram_tensor("y", [TL, D], F32, kind="ExternalOutput").ap()
    with ExitStack() as es:
        kb = KB(nc, es)
        yT_db, h_db, hnT_db, y_db, fr_db = Buf("yT_d"), Buf("h_d"), Buf("hnT_d"), Buf("y_d"), Buf("front")
        emit_front(kb, xall, x, wblk, g1, hg, bfg, qT_d, kall_d, vall_d, lfall_d, fr_db)
        kb.begin_phase()
        emit_attention(kb, qT_d, kall_d, vall_d, lfall_d, cst, yT_d, yT_db)
        kb.end_phase()
        emit_mid(kb, x, g1, w_g, w_up_a, w_up_b, w_out, g2, yT_d, yT_db, merged_d, h_d, h_db, hnT_d, hnT_db)
        emit_peer(kb, hnT_d, hnT_db, h_d, h_db, w_pq, skT, uT, v, iota_d, sc_d, G_d, act_d, y, y_db)
        kb.drain(kb.sp, [y_db])
    return nc


def kernel_fused(**inp):
    f32 = lambda a: np.asarray(a, np.float32)
    x = f32(inp["x"]).reshape(S, D)
    xb = x.reshape(S // 128, 128, D)
    xs = [np.ascontiguousarray(xb[c::NCORES].reshape(TL, D)) for c in range(NCORES)]
    w_in = f32(inp["w_in"])[0]
    wblk = host_wblk(w_in)
    w_g = np.ascontiguousarray(np.stack(
        [w_in[:, c0:c0 + D].reshape(NDC, 128, 16, 256).transpose(2, 1, 0, 3) for c0 in (C_GA, C_GB)], axis=2).reshape(16, 128, 2 * NDC * 256))
    g1 = np.ascontiguousarray(f32(inp["norm1_gain"])[0].reshape(NDC, 128).T)
    g2 = np.ascontiguousarray(f32(inp["norm2_gain"])[0].reshape(NDC, 128).T)
    hg = np.ascontiguousarray(np.stack([f32(inp[k])[0] for k in ("q_norm_a", "k_norm_a", "q_norm_b", "k_norm_b")], axis=1))
    bfg = np.ascontiguousarray(np.broadcast_to(f32(inp["b_forget"])[0][None, :], (128, NHB)))
    u3 = f32(inp["peer_u"])[0].reshape(128, 128, D)
    uT = np.ascontiguousarray(u3.reshape(128, 32, 4, NDC, 128).transpose(1, 4, 3, 2, 0).reshape(32, 128, NDC * 512))
    v = np.ascontiguousarray(f32(inp["peer_v"])[0].reshape(128, 16, 8, 8, 512).transpose(3, 1, 0, 2, 4).reshape(8, 16, 128, 8 * 512))
    skT = np.ascontiguousarray(f32(inp["peer_subkeys"])[0].reshape(16, 128, 128).transpose(0, 2, 1))
    iota = np.ascontiguousarray(np.broadcast_to(np.arange(128, dtype=np.float32)[None, :], (128, 128)))
    common = {"wblk": wblk, "hg": hg, "bfg": bfg, "g1": g1, "g2": g2, "w_g": w_g,
              "w_up_a": np.ascontiguousarray(f32(inp["w_up_a"])[0]), "w_up_b": np.ascontiguousarray(f32(inp["w_up_b"])[0]),
              "w_out": np.ascontiguousarray(f32(inp["w_out"])[0]), "w_pq": np.ascontiguousarray(f32(inp["w_peer_q"])[0]),
              "skT": skT, "uT": uT, "v": v, "iota": iota}
    in_maps = []
    for c in range(NCORES):
        m = dict(common)
        m["x"] = xs[c]
        m["xall"] = np.ascontiguousarray(np.concatenate([xs[(c - r) % NCORES] for r in range(NCORES)], axis=0))
        m.update(core_consts(c))
        in_maps.append(m)
    nc = build_fused()
    r = run_bass_kernel_spmd(nc, in_maps, core_ids=list(range(NCORES))).results
    out = np.empty((S // 128, 128, D), np.float32)
    for c in range(NCORES):
        out[c::NCORES] = np.asarray(r[c]["y"], np.float32).reshape(NTT, 128, D)
    return out.reshape(1, S, D)


kernel = kernel_fused
```
